# Optimizing a Trainium2 kernel written in Bass

```python
import math
import jax, jax.numpy as jnp
from jax import lax
import numpy as np

D_MODEL = 2048
BATCH = 4
SEQ = 4096
DEPTH = 1

HEAD_DIM = 64
DILATION_PATTERNS = ((128, 1), (512, 4), (2048, 16))
N_ATT_GROUPS = len(DILATION_PATTERNS)
HEADS_PER_GROUP = 6
N_Q_HEADS = N_ATT_GROUPS * HEADS_PER_GROUP
KV_HEADS = HEADS_PER_GROUP
ATT_Q_WIDTH = N_Q_HEADS * HEAD_DIM
KV_WIDTH = KV_HEADS * HEAD_DIM
ROT_DIM = HEAD_DIM // 4
ROPE_THETA = 500000.0
BLK = 128
SSM_WIDTH = D_MODEL - ATT_Q_WIDTH
SSM_GROUP_CH = 16
SSM_GROUPS = SSM_WIDTH // SSM_GROUP_CH
SSM_STATE = 64
IN_WIDTH = ATT_Q_WIDTH + 2 * KV_WIDTH + SSM_WIDTH
OUT_IN_WIDTH = KV_WIDTH + SSM_WIDTH
D_FF = 4 * D_MODEL
N_MOD = 6
EPS = 1e-6

kernel_name = "hymba_s5_longnet_sandwich_adaln_block"


def rms_norm(x, g):
    xf = x.astype(jnp.float32)
    y = xf * lax.rsqrt(jnp.mean(xf * xf, axis=-1, keepdims=True) + EPS)
    return (y * g.astype(jnp.float32)).astype(x.dtype)


def rope_partial(t, positions):
    freqs = ROPE_THETA ** (-jnp.arange(0, ROT_DIM, 2, dtype=jnp.float32) / ROT_DIM)
    ang = positions.astype(jnp.float32)[..., None] * freqs
    cos = jnp.cos(ang)[:, :, None, :]
    sin = jnp.sin(ang)[:, :, None, :]
    tr = t[..., :ROT_DIM].astype(jnp.float32)
    x1, x2 = tr[..., :ROT_DIM // 2], tr[..., ROT_DIM // 2:]
    rot = jnp.concatenate([x1 * cos - x2 * sin, x2 * cos + x1 * sin], axis=-1).astype(t.dtype)
    return jnp.concatenate([rot, t[..., ROT_DIM:]], axis=-1)


def dilated_window_attention(q, k, v, dilation, span):
    assert span <= BLK
    B, L, H, Dh = q.shape
    n = L // dilation

    def to_sub(t):
        return t.reshape(B, n, dilation, H, Dh).transpose(0, 2, 3, 1, 4)

    qs, ks, vs = to_sub(q), to_sub(k), to_sub(v)
    n_pad = -(-n // BLK) * BLK
    pad = ((0, 0), (0, 0), (0, 0), (0, n_pad - n), (0, 0))
    qs, ks, vs = (jnp.pad(t, pad) for t in (qs, ks, vs))
    nb = n_pad // BLK
    qb = qs.reshape(B, dilation, H, nb, BLK, Dh)
    kb = ks.reshape(B, dilation, H, nb, BLK, Dh)
    vb = vs.reshape(B, dilation, H, nb, BLK, Dh)

    def with_prev(t):
        prev = jnp.concatenate([jnp.zeros_like(t[:, :, :, :1]), t[:, :, :, :-1]], axis=3)
        return jnp.concatenate([prev, t], axis=4)

    kc, vc = with_prev(kb), with_prev(vb)
    scores = jnp.einsum('brhnqe,brhnke->brhnqk', qb, kc).astype(jnp.float32) / math.sqrt(Dh)
    blk = jnp.arange(nb)[:, None, None]
    qi = jnp.arange(BLK)[None, :, None]
    kj = jnp.arange(2 * BLK)[None, None, :]
    dist = qi + BLK - kj
    valid = (dist >= 0) & (dist <= span) & (blk * BLK - BLK + kj >= 0)
    scores = jnp.where(valid, scores, -jnp.inf)
    lse = jax.nn.logsumexp(scores, axis=-1)
    p = jnp.exp(scores - lse[..., None]).astype(v.dtype)
    out = jnp.einsum('brhnqk,brhnke->brhnqe', p, vc)
    out = out.reshape(B, dilation, H, n_pad, Dh)[:, :, :, :n]
    out = out.transpose(0, 3, 1, 2, 4).reshape(B, L, H, Dh)
    lse = lse.reshape(B, dilation, H, n_pad)[:, :, :, :n].transpose(0, 3, 1, 2).reshape(B, L, H)
    return out, lse


def _scan_op(e1, e2):
    a1, b1 = e1
    a2, b2 = e2
    return a1 * a2, a2 * b1 + b2


def s5_mixer(u, a_re, a_im, log_dt, b_re, b_im, c_re, c_im, d_skip, w_glu, b_glu):
    B, L, G, P = u.shape
    u32 = u.astype(jnp.float32)
    lam = lax.complex(a_re.astype(jnp.float32), a_im.astype(jnp.float32))
    dt = jnp.exp(log_dt.astype(jnp.float32))[:, None]
    a_bar = jnp.exp(lam * dt)
    b_mat = lax.complex(b_re.astype(jnp.float32), b_im.astype(jnp.float32))
    b_bar = ((a_bar - 1.0) / lam)[..., None] * b_mat
    c_mat = lax.complex(c_re.astype(jnp.float32), c_im.astype(jnp.float32))
    bu = jnp.einsum('blgp,gnp->blgn', u32.astype(jnp.complex64), b_bar)
    a_all = jnp.broadcast_to(a_bar, bu.shape)
    _, state = lax.associative_scan(_scan_op, (a_all, bu), axis=1)
    y = jnp.einsum('blgn,gpn->blgp', state, c_mat).real + d_skip.astype(jnp.float32) * u32
    y = y.reshape(B, L, G * P)
    y = jax.nn.gelu(y)
    y = y * jax.nn.sigmoid(y @ w_glu.astype(jnp.float32) + b_glu.astype(jnp.float32))
    return y.astype(u.dtype)


def setup_inputs(seed: int = 0) -> dict:
    key = jax.random.key(seed)
    ks = jax.random.split(key, 32)
    f32 = jnp.float32
    nrm = lambda k, shape, s: jax.random.normal(k, shape, f32) * s
    x = jax.random.normal(ks[0], (BATCH, SEQ, D_MODEL), f32)
    c = jax.random.normal(ks[1], (BATCH, D_MODEL), f32)
    offset = jax.random.randint(ks[2], (BATCH, 1), 0, 1024, dtype=jnp.int32)
    positions = (offset + jnp.arange(SEQ, dtype=jnp.int32)[None, :]).astype(jnp.int32)
    n_idx = jnp.arange(SSM_STATE, dtype=f32)
    return {
        "x": x,
        "c": c,
        "positions": positions,
        "w_ada": nrm(ks[3], (DEPTH, D_MODEL, N_MOD * D_MODEL), 0.5 * D_MODEL ** -0.5),
        "b_ada": nrm(ks[4], (DEPTH, N_MOD * D_MODEL), 0.01),
        "g_pre_mix": 1.0 + nrm(ks[5], (DEPTH, D_MODEL), 0.02),
        "g_post_mix": 1.0 + nrm(ks[6], (DEPTH, D_MODEL), 0.02),
        "w_in": nrm(ks[7], (DEPTH, D_MODEL, IN_WIDTH), D_MODEL ** -0.5),
        "ssm_a_re": -0.5 + nrm(ks[8], (DEPTH, SSM_GROUPS, SSM_STATE), 0.01),
        "ssm_a_im": math.pi * n_idx + nrm(ks[9], (DEPTH, SSM_GROUPS, SSM_STATE), 0.01),
        "ssm_log_dt": jax.random.uniform(ks[10], (DEPTH, SSM_GROUPS), f32, math.log(1e-3), math.log(1e-1)),
        "ssm_b_re": nrm(ks[11], (DEPTH, SSM_GROUPS, SSM_STATE, SSM_GROUP_CH), (2 * SSM_GROUP_CH) ** -0.5),
        "ssm_b_im": nrm(ks[12], (DEPTH, SSM_GROUPS, SSM_STATE, SSM_GROUP_CH), (2 * SSM_GROUP_CH) ** -0.5),
        "ssm_c_re": nrm(ks[13], (DEPTH, SSM_GROUPS, SSM_GROUP_CH, SSM_STATE), (2 * SSM_STATE) ** -0.5),
        "ssm_c_im": nrm(ks[14], (DEPTH, SSM_GROUPS, SSM_GROUP_CH, SSM_STATE), (2 * SSM_STATE) ** -0.5),
        "ssm_d": nrm(ks[15], (DEPTH, SSM_GROUPS, SSM_GROUP_CH), 1.0),
        "w_glu": nrm(ks[16], (DEPTH, SSM_WIDTH, SSM_WIDTH), SSM_WIDTH ** -0.5),
        "b_glu": nrm(ks[17], (DEPTH, SSM_WIDTH), 0.01),
        "g_attn_out": 1.0 + nrm(ks[18], (DEPTH, KV_WIDTH), 0.02),
        "g_ssm_out": 1.0 + nrm(ks[19], (DEPTH, SSM_WIDTH), 0.02),
        "w_out": nrm(ks[20], (DEPTH, OUT_IN_WIDTH, D_MODEL), OUT_IN_WIDTH ** -0.5),
        "g_pre_mlp": 1.0 + nrm(ks[21], (DEPTH, D_MODEL), 0.02),
        "g_post_mlp": 1.0 + nrm(ks[22], (DEPTH, D_MODEL), 0.02),
        "w_mlp_in": nrm(ks[23], (DEPTH, D_MODEL, D_FF), D_MODEL ** -0.5),
        "w_mlp_out": nrm(ks[24], (DEPTH, D_FF, D_MODEL), D_FF ** -0.5),
    }


def reference(x, c, positions, w_ada, b_ada, g_pre_mix, g_post_mix, w_in,
              ssm_a_re, ssm_a_im, ssm_log_dt, ssm_b_re, ssm_b_im, ssm_c_re, ssm_c_im,
              ssm_d, w_glu, b_glu, g_attn_out, g_ssm_out, w_out,
              g_pre_mlp, g_post_mlp, w_mlp_in, w_mlp_out):
    B, L, _ = x.shape
    for l in range(DEPTH):
        mod = jax.nn.silu(c) @ w_ada[l] + b_ada[l]
        sh1, sc1, gt1, sh2, sc2, gt2 = (m[:, None, :] for m in jnp.split(mod, N_MOD, axis=-1))

        h = rms_norm(x, g_pre_mix[l]) * (1.0 + sc1) + sh1
        proj = h @ w_in[l]
        q = proj[..., :ATT_Q_WIDTH].reshape(B, L, N_Q_HEADS, HEAD_DIM)
        k = proj[..., ATT_Q_WIDTH:ATT_Q_WIDTH + KV_WIDTH].reshape(B, L, KV_HEADS, HEAD_DIM)
        v = proj[..., ATT_Q_WIDTH + KV_WIDTH:ATT_Q_WIDTH + 2 * KV_WIDTH].reshape(B, L, KV_HEADS, HEAD_DIM)
        u = proj[..., ATT_Q_WIDTH + 2 * KV_WIDTH:].reshape(B, L, SSM_GROUPS, SSM_GROUP_CH)

        q = rope_partial(q, positions).reshape(B, L, N_ATT_GROUPS, HEADS_PER_GROUP, HEAD_DIM)
        k = rope_partial(k, positions)
        outs, lses = [], []
        for gi, (window, dilation) in enumerate(DILATION_PATTERNS):
            o_g, lse_g = dilated_window_attention(q[:, :, gi], k, v, dilation, window // dilation)
            outs.append(o_g)
            lses.append(lse_g)
        wts = jax.nn.softmax(jnp.stack(lses, axis=0), axis=0)
        att = jnp.sum(wts[..., None].astype(x.dtype) * jnp.stack(outs, axis=0), axis=0)
        att = rms_norm(att.reshape(B, L, KV_WIDTH), g_attn_out[l])

        ssm = s5_mixer(u, ssm_a_re[l], ssm_a_im[l], ssm_log_dt[l], ssm_b_re[l], ssm_b_im[l],
                       ssm_c_re[l], ssm_c_im[l], ssm_d[l], w_glu[l], b_glu[l])
        ssm = rms_norm(ssm, g_ssm_out[l])

        mix = jnp.concatenate([att, ssm], axis=-1) @ w_out[l]
        x = x + gt1 * rms_norm(mix, g_post_mix[l])

        h = rms_norm(x, g_pre_mlp[l]) * (1.0 + sc2) + sh2
        y = jnp.square(jax.nn.relu(h @ w_mlp_in[l])) @ w_mlp_out[l]
        x = x + gt2 * rms_norm(y, g_post_mlp[l])
    return x
```

```python
import contextlib
import numpy as np
import concourse.bass as bass
import concourse.mybir as mybir
from concourse.bass_utils import run_bass_kernel_spmd

F32 = mybir.dt.float32
BF16 = mybir.dt.bfloat16
I32 = mybir.dt.int32
AF = mybir.ActivationFunctionType
ALU = mybir.AluOpType

D = 2048
NT = 512
NCH = 8
PI = float(np.pi)
MAGIC = 12582912.0
NEG = -30000.0
EPS = 1e-6


class Prog:
    NDMA = 8

    def __init__(self, nc, es):
        self.nc = nc
        self.eh = dict(pe=nc.tensor, act=nc.scalar, dve=nc.vector, pool=nc.gpsimd, sp=nc.sync)
        self.sem = {e: es.enter_context(nc.semaphore("s_" + e)) for e in self.eh}
        self.cnt = {e: 0 for e in self.eh}
        self.seen = {e: {} for e in self.eh}
        self.dsem = {}
        self.dcnt = {}
        self.drr = {}
        self.semobj = {}
        for e in self.eh:
            self.semobj["s_" + e] = self.sem[e]
        for q in ("sp", "pool"):
            self.dsem[q] = [es.enter_context(nc.semaphore("d_%s%d" % (q, i))) for i in range(self.NDMA)]
            self.dcnt[q] = [0] * self.NDMA
            self.drr[q] = 0
            for i, s in enumerate(self.dsem[q]):
                self.semobj["d_%s%d" % (q, i)] = s
        self.lastw = {}
        self.readers = {}
        self.nwait = 0
        self.nins = 0
        self.dead = False

    def _need(self, eng, reads, writes):
        need = {}

        def add(t, same_ok):
            if t is None:
                return
            sn, val, e = t
            if same_ok and e == eng and eng == "pe":
                return
            if need.get(sn, 0) < val:
                need[sn] = val
        for k in reads:
            add(self.lastw.get(k), False)
        for k in writes:
            add(self.lastw.get(k), True)
            for sn, (val, e) in self.readers.get(k, {}).items():
                add((sn, val, e), True)
        return need

    def _emit_waits(self, eng, need):
        h = self.eh[eng]
        seen = self.seen[eng]
        for sn, val in need.items():
            if seen.get(sn, 0) >= val:
                continue
            h.wait_ge(self.semobj[sn], val)
            seen[sn] = val
            self.nwait += 1

    def _record(self, ticket, reads, writes):
        sn, val, e = ticket
        for k in reads:
            d = self.readers.setdefault(k, {})
            if d.get(sn, (0, None))[0] < val:
                d[sn] = (val, e)
        for k in writes:
            self.lastw[k] = ticket
            self.readers[k] = {}

    def op(self, eng, fn, reads=(), writes=(), sig=True):
        if self.dead:
            return None
        psr = [k for k in reads if isinstance(k, tuple) and k[0] == "ps"]
        if psr:
            writes = list(writes) + psr
        need = self._need(eng, reads, writes)
        self._emit_waits(eng, need)
        ins = fn()
        self.nins += 1
        sn = "s_" + eng
        if sig:
            self.cnt[eng] += 1
            ins.then_inc(self.sem[eng], 1)
            ticket = (sn, self.cnt[eng], eng)
        else:
            ticket = (sn, self.cnt[eng] + 1, eng)
        self._record(ticket, reads, writes)
        return ins

    def dma(self, q, out, in_, reads=(), writes=()):
        if self.dead:
            return None
        need = self._need(q, reads, writes)
        i = self.drr[q]
        self.drr[q] = (i + 1) % self.NDMA
        sn = "d_%s%d" % (q, i)
        if self.dcnt[q][i] > 0 and need.get(sn, 0) < self.dcnt[q][i]:
            need[sn] = self.dcnt[q][i]
        self._emit_waits(q, need)
        ins = self.eh[q].dma_start(out=out, in_=in_)
        self.dcnt[q][i] += 16
        ins.then_inc(self.dsem[q][i], 16)
        self.nins += 1
        self._record((sn, self.dcnt[q][i], None), reads, writes)
        return ins

    def barrier(self):
        if self.dead:
            return
        need = {}
        for e in self.eh:
            if self.cnt[e] > 0:
                need["s_" + e] = self.cnt[e]
        for q in self.dsem:
            for i in range(self.NDMA):
                if self.dcnt[q][i] > 0:
                    need["d_%s%d" % (q, i)] = self.dcnt[q][i]
        for e in self.eh:
            n2 = {k: v for k, v in need.items() if k != "s_" + e}
            self._emit_waits(e, n2)
        self.lastw = {}
        self.readers = {}


def build_program(debug=(), stop_after=None):
    nc = bass.Bass("TRN2", target_bir_lowering=False)
    dbg = {}

    def din(name, shape, dt=F32):
        return nc.dram_tensor(name, list(shape), dt, kind="ExternalInput").ap()

    x_own = din("x_own", [2048, D])
    x_prev = din("x_prev", [2048, D])
    pos_in = din("pos", [1, 4096], I32)
    c_col_in = din("c_col", [128, 16])
    cc_in = din("cc", [128, 8])
    masks_in = din("masks", [128, 4 * 128])
    ident_in = din("ident", [128, 128])
    rmat_in = din("rmat", [128, 128])
    jidx_in = din("jidx", [128, NT])
    w_ada_in = din("w_ada_t", [24, 128, 16 * 512])
    b_ada_in = din("b_ada_col", [128, 96])
    gcols_in = din("gcols", [128, 4 * 16 + 3 + 7 + 7 + 7])
    w_in_in = din("w_in_t", [22, 128, 16 * 128])
    w_out_in = din("w_out_t", [16, 128, 10 * 128])
    w_m1_in = din("w_m1_t", [64, 128, 16 * 128])
    w_m2_in = din("w_m2_t", [16, 128, 64 * 128])
    w_glu_in = din("w_glu_t", [7, 128, 7 * 128])
    alay_in = din("alay", [128, 3 * 1792])
    bexp_in = din("bexp", [128, 2 * 1792])
    cexp_in = din("cexp", [128, 2 * 3584])
    lanes_in = din("lanes", [128, 3 * 56])
    out = nc.dram_tensor("out", [2048, D], F32, kind="ExternalOutput").ap()
    catT = nc.dram_tensor("catT_scr", [10, 128, 2048], BF16, kind="Internal").ap()
    uT_scr = nc.dram_tensor("uT_scr", [7, 128, 4096], BF16, kind="Internal").ap()
    w_m1_b = nc.dram_tensor("w_m1_b", [64, 128, 16 * 128], BF16, kind="Internal").ap()
    w_m2_b = nc.dram_tensor("w_m2_b", [16, 128, 64 * 128], BF16, kind="Internal").ap()
    w_out_b = nc.dram_tensor("w_out_b", [16, 128, 10 * 128], BF16, kind="Internal").ap()

    def dout(name, shape, dt=F32):
        dbg[name] = nc.dram_tensor("dbg_" + name, list(shape), dt, kind="ExternalOutput").ap()
        return dbg[name]

    def ckpt(name):
        if stop_after == name:
            p.dead = True

    with contextlib.ExitStack() as es:
        p = Prog(nc, es)
        _phases(nc, es, p, ckpt, debug, dout, locals())
        p.dead = False
        p.barrier()
    build_program.stats = (p.nins, p.nwait)
    return nc, dbg


def _phases(nc, es, p, ckpt, debug, dout, env):
    globals_needed = ("x_own", "x_prev", "pos_in", "c_col_in", "cc_in", "masks_in", "ident_in", "rmat_in", "jidx_in",
                      "w_ada_in", "b_ada_in", "gcols_in", "w_in_in", "w_out_in", "w_m1_in", "w_m2_in", "w_glu_in",
                      "alay_in", "bexp_in", "cexp_in", "lanes_in", "out", "catT", "w_m1_b", "w_m2_b", "w_out_b", "uT_scr")
    (x_own, x_prev, pos_in, c_col_in, cc_in, masks_in, ident_in, rmat_in, jidx_in, w_ada_in, b_ada_in, gcols_in,
     w_in_in, w_out_in, w_m1_in, w_m2_in, w_glu_in, alay_in, bexp_in, cexp_in, lanes_in, out, catT, w_m1_b, w_m2_b, w_out_b, uT_scr) = (env[k] for k in globals_needed)
    if True:
        _tn = [0]

        def T(st, name, shape, dt):
            _tn[0] += 1
            return st.enter_context(nc.sbuf_tensor("t%d_%s" % (_tn[0], name), list(shape), dt))
        ps = [es.enter_context(nc.psum_tensor("psb%d" % i, [128, 512], F32)) for i in range(8)]
        psb = [t[:].bitcast(BF16) for t in ps]
        PS = lambda i: ("ps", i)

        ident_f = T(es, "ident_f", [128, 128], F32)
        ident_b = T(es, "ident_b", [128, 128], BF16)
        ones_b = T(es, "ones_b", [128, 128], BF16)
        rmat_b = T(es, "rmat_b", [128, 128], BF16)
        masks_b = T(es, "masks_b", [128, 512], BF16)
        cc = T(es, "cc", [128, 8], F32)
        jidx = T(es, "jidx", [128, NT], F32)
        gcols = T(es, "gcols", [128, 88], F32)
        modc = T(es, "modc", [128, 96], F32)
        p.dma("sp", ident_f[:], ident_in[:, :], writes=["ident_f"])
        p.dma("pool", ident_b[:], ident_in[:, :], writes=["ident_b"])
        p.dma("pool", rmat_b[:], rmat_in[:, :], writes=["rmat_b"])
        p.dma("pool", masks_b[:], masks_in[:, :], writes=["masks_b"])
        p.dma("sp", cc[:], cc_in[:, :], writes=["cc"])
        p.dma("sp", jidx[:], jidx_in[:, :], writes=["jidx"])
        p.dma("sp", gcols[:], gcols_in[:, :], writes=["gcols"])
        p.op("dve", lambda: nc.vector.memset(ones_b[:], 1.0), writes=["ones_b"])
        conv_list = [(w_out_b[i], w_out_in[i], ("cv_out", i)) for i in range(16)] + \
                    [(w_m1_b[i], w_m1_in[i], ("cv_m1", i)) for i in range(64)] + \
                    [(w_m2_b[i], w_m2_in[i], ("cv_m2", i)) for i in range(16)]
        conv_pos = [0]

        def conv_step(n=1):
            for _ in range(n):
                if conv_pos[0] < len(conv_list):
                    dst, src, key = conv_list[conv_pos[0]]
                    conv_pos[0] += 1
                    p.dma("pool", dst, src, writes=[key])
        FREQT, RM, ORM, SG, FLAG = 0, 1, 2, 3, 4
        G_PRE_MIX, G_POST_MIX, G_PRE_MLP, G_POST_MLP, G_ATT, G_SSM, B_GLU, DSK = 0, 16, 32, 48, 64, 67, 74, 81
        A1, B1, G1, A2, B2, G2 = 0, 16, 32, 48, 64, 80

        with contextlib.ExitStack() as ph:
            c_col = T(ph, "c_col", [128, 16], F32)
            s_col = T(ph, "s_col", [128, 16], BF16)
            b_ada = T(ph, "b_ada", [128, 96], F32)
            modr = T(ph, "modr", [128, 96], F32)
            wa = [T(ph, "wa%d" % i, [128, 16, 512], BF16) for i in range(2)]
            p.dma("sp", c_col[:], c_col_in[:, :], writes=["c_col"])
            p.dma("sp", b_ada[:], b_ada_in[:, :], writes=["b_ada"])
            p.op("act", lambda: nc.scalar.activation(s_col[:], c_col[:], AF.Silu), reads=["c_col"], writes=["s_col"])
            for mg in range(24):
                w = wa[mg % 2]
                p.dma("pool", w[:], w_ada_in[mg].rearrange("p (k m) -> p k m", k=16), writes=[("wa", mg % 2)])
                for m4 in range(4):
                    col = mg * 4 + m4
                    for kt in range(16):
                        p.op("pe", lambda w=w, m4=m4, kt=kt, col=col: nc.tensor.matmul(
                            ps[0][:, col:col + 1], w[:, kt, m4 * 128:(m4 + 1) * 128], s_col[:, kt:kt + 1],
                            start=(kt == 0), stop=(kt == 15)),
                            reads=[("wa", mg % 2), "s_col"], writes=[PS(0)] if (mg == 0 and m4 == 0 and kt == 0) else [],
                            sig=(kt == 15))
            p.lastw[PS(0)] = ("s_pe", p.cnt["pe"], "pe")
            p.op("dve", lambda: nc.vector.tensor_tensor(modr[:], ps[0][:, 0:96], b_ada[:], ALU.add),
                 reads=[PS(0), "b_ada"], writes=["modr"])
            for (dst, gsrc, col) in ((A1, G_PRE_MIX, 16), (A2, G_PRE_MLP, 64)):
                p.op("dve", lambda dst=dst, gsrc=gsrc, col=col: nc.vector.scalar_tensor_tensor(
                    modc[:, dst:dst + 16], modr[:, col:col + 16], 1.0, gcols[:, gsrc:gsrc + 16], ALU.add, ALU.mult),
                    reads=["modr", "gcols"], writes=["modc"])
            for (dst, col) in ((B1, 0), (B2, 48)):
                p.op("dve", lambda dst=dst, col=col: nc.vector.tensor_copy(modc[:, dst:dst + 16], modr[:, col:col + 16]),
                     reads=["modr"], writes=["modc"])
            for (dst, gsrc, col) in ((G1, G_POST_MIX, 32), (G2, G_POST_MLP, 80)):
                p.op("dve", lambda dst=dst, gsrc=gsrc, col=col: nc.vector.tensor_tensor(
                    modc[:, dst:dst + 16], modr[:, col:col + 16], gcols[:, gsrc:gsrc + 16], ALU.mult),
                    reads=["modr", "gcols"], writes=["modc"])
            if "mod" in debug:
                p.dma("sp", dout("mod", [128, 96])[:, :], modr[:], reads=["modr"], writes=["o_mod"])
                p.dma("sp", dout("modc", [128, 96])[:, :], modc[:], reads=["modc"], writes=["o_modc"])
            p.barrier()
            ckpt("p0")

        def hT_steps(st_tiles, ci, acol, bcol, hkey="hT"):
            xt, sqj, xn, rs, hT = st_tiles
            src = x_prev if ci < 4 else x_own
            steps = []

            def s1a(tt):
                row0 = (ci % 4) * NT + tt * 128
                p.dma("sp", xt[tt % 2][:], src[row0:row0 + 128, :], writes=[("xt", tt % 2)])

            def s1(tt):
                xb_ = xt[tt % 2]
                p.op("act", lambda: nc.scalar.activation(sqj[:], xb_[:], AF.Square, accum_out=rs[:, 4 * (tt % 2):4 * (tt % 2) + 1]),
                     reads=[("xt", tt % 2)], writes=["sqj", ("rs0", tt % 2)])
                p.op("act", lambda: nc.scalar.activation(rs[:, 4 * (tt % 2) + 1:4 * (tt % 2) + 2], rs[:, 4 * (tt % 2):4 * (tt % 2) + 1], AF.Ln, bias=EPS, scale=1.0 / D),
                     reads=[("rs0", tt % 2)], writes=[("rs1", tt % 2)])
                p.op("act", lambda: nc.scalar.activation(rs[:, 4 * (tt % 2) + 2:4 * (tt % 2) + 3], rs[:, 4 * (tt % 2) + 1:4 * (tt % 2) + 2], AF.Exp, scale=-0.5),
                     reads=[("rs1", tt % 2)], writes=[("rs2", tt % 2)])

            def s2(tt):
                xb_ = xt[tt % 2]
                p.op("dve", lambda: nc.vector.tensor_scalar(xn[:], xb_[:], rs[:, 4 * (tt % 2) + 2:4 * (tt % 2) + 3], None, ALU.mult),
                     reads=[("xt", tt % 2), ("rs2", tt % 2)], writes=["xn"])

            def s3(tt, half, part):
                if True:
                    bank = half
                    if part == 0:
                        for f8 in range(8):
                            f = half * 8 + f8
                            p.op("pe", lambda f=f, f8=f8: nc.tensor.transpose(
                                psb[bank][:, f8 * 128:(f8 + 1) * 128], xn[:, f * 128:(f + 1) * 128], ident_b[:]),
                                reads=["xn", "ident_b"], writes=[PS(bank)] if f8 == 0 else [], sig=(f8 == 7))
                        p.lastw[PS(bank)] = ("s_pe", p.cnt["pe"], "pe")
                    for f8 in range(4 * part, 4 * part + 4):
                        f = half * 8 + f8
                        p.op("act", lambda f=f, f8=f8: nc.scalar.activation(
                            hT[:, f, tt * 128:(tt + 1) * 128], psb[bank][:, f8 * 128:(f8 + 1) * 128], AF.Identity,
                            bias=modc[:, bcol + f:bcol + f + 1], scale=modc[:, acol + f:acol + f + 1]),
                            reads=[PS(bank), "modc"], writes=[(hkey, f)])
            steps.append(lambda: s1a(0))
            steps.append(lambda: s1a(1))
            for tt in range(4):
                steps.append(lambda tt=tt: s1(tt))
                steps.append(lambda tt=tt: s2(tt))
                steps.append(lambda tt=tt: s3(tt, 0, 0))
                steps.append(lambda tt=tt: s3(tt, 0, 1))
                steps.append(lambda tt=tt: s3(tt, 1, 0))
                steps.append(lambda tt=tt: s3(tt, 1, 1))
                if tt + 2 < 4:
                    steps.append(lambda tt=tt: s1a(tt + 2))
            return steps

        def make_hT(st_tiles, ci, acol, bcol):
            for st_ in hT_steps(st_tiles, ci, acol, bcol):
                st_()

        def hT_tiles(st):
            xt = [T(st, "xt%d" % i, [128, D], F32) for i in range(2)]
            sqj = T(st, "sqj", [128, D], BF16)
            xn = T(st, "xn", [128, D], BF16)
            rs = T(st, "rs", [128, 8], F32)
            hT = T(st, "hT", [128, 16, NT], BF16)
            return (xt, sqj, xn, rs, hT)

        def rstd_rep(dst, src_ps, n, key_src, key_dst, tmp, key_tmp):
            p.op("act", lambda: nc.scalar.activation(tmp[:], src_ps, AF.Ln, bias=EPS, scale=1.0 / n),
                 reads=[key_src], writes=[key_tmp])
            p.op("act", lambda: nc.scalar.activation(dst[:], tmp[:], AF.Exp, scale=-0.5),
                 reads=[key_tmp], writes=[key_dst])

        with contextlib.ExitStack() as pa:
            qT = T(pa, "qT", [128, 9, 2048], BF16)
            kT = T(pa, "kT", [128, 3, 4096], BF16)
            vT = T(pa, "vT", [128, 3, 4096], BF16)
            with contextlib.ExitStack() as ph:
                tiles0 = hT_tiles(ph)
                hT_b = [tiles0[4], T(ph, "hT1", [128, 16, NT], BF16)]
                wq = [T(ph, "wq%d" % i, [128, 3, 16, 128], BF16) for i in range(2)]
                posi = T(ph, "posi", [128, NT], I32)
                ra = T(ph, "ra", [128, NT], F32)
                rb = T(ph, "rb", [128, NT], F32)
                rS = T(ph, "rS", [128, NT], F32)
                rC = T(ph, "rC", [128, NT], F32)
                cosk_b = [T(ph, "cosk%d" % i, [128, NT], F32) for i in range(2)]
                sink_b = [T(ph, "sink%d" % i, [128, NT], F32) for i in range(2)]
                cosq_b = [T(ph, "cosq%d" % i, [128, NT], F32) for i in range(2)]
                sinq_b = [T(ph, "sinq%d" % i, [128, NT], F32) for i in range(2)]
                xbf = [T(ph, "xbf%d" % i, [128, NT], BF16) for i in range(2)]
                t1 = [T(ph, "t1_%d" % i, [128, NT], F32) for i in range(2)]
                t2 = [T(ph, "t2_%d" % i, [128, NT], F32) for i in range(2)]

                def rope_steps(ci_):
                    pr = ci_ % 2
                    cosk, sink, cosq, sinq = cosk_b[pr], sink_b[pr], cosq_b[pr], sinq_b[pr]
                    ck, sk, cq, sq_ = ("cosk", pr), ("sink", pr), ("cosq", pr), ("sinq", pr)

                    def r1():
                        p.dma("sp", posi[:], pos_in[0:1, ci_ * NT:(ci_ + 1) * NT].to_broadcast([128, NT]), writes=["posi"])

                    def r2():
                        p.op("act", lambda: nc.scalar.activation(ra[:], posi[:], AF.Identity, scale=cc[:, FREQT:FREQT + 1]),
                             reads=["posi", "cc"], writes=["ra"])
                        p.op("dve", lambda: nc.vector.tensor_scalar(rb[:], ra[:], MAGIC, None, ALU.add), reads=["ra"], writes=["rb"])
                        p.op("dve", lambda: nc.vector.scalar_tensor_tensor(rb[:], rb[:], MAGIC, ra[:], ALU.subtract, ALU.subtract),
                             reads=["rb", "ra"], writes=["rb"])

                    def r3():
                        p.op("act", lambda: nc.scalar.activation(rS[:], rb[:], AF.Sin, scale=-2.0 * PI), reads=["rb"], writes=["rS"])
                        p.op("act", lambda: nc.scalar.activation(ra[:], rb[:], AF.Abs), reads=["rb"], writes=["ra"])
                        p.op("act", lambda: nc.scalar.activation(rC[:], ra[:], AF.Sin, scale=-2.0 * PI, bias=cc[:, 5:6]),
                             reads=["ra", "cc"], writes=["rC"])

                    def r4():
                        p.op("dve", lambda: nc.vector.tensor_scalar(cosk[:], rC[:], cc[:, RM:RM + 1], cc[:, ORM:ORM + 1], ALU.mult, ALU.add),
                             reads=["rC", "cc"], writes=[ck])
                        p.op("dve", lambda: nc.vector.tensor_scalar(sink[:], rS[:], cc[:, SG:SG + 1], None, ALU.mult),
                             reads=["rS", "cc"], writes=[sk])
                        if ci_ >= 4:
                            p.op("dve", lambda: nc.vector.tensor_scalar(cosq[:], cosk[:], 0.125, None, ALU.mult), reads=[ck], writes=[cq])
                            p.op("dve", lambda: nc.vector.tensor_scalar(sinq[:], sink[:], 0.125, None, ALU.mult), reads=[sk], writes=[sq_])
                    return [r1, r2, r3, r4]

                def prep1a(ci_):
                    tl = (tiles0[0], tiles0[1], tiles0[2], tiles0[3], hT_b[ci_ % 2])
                    return hT_steps(tl, ci_, A1, B1, hkey=("hT", ci_ % 2)) + rope_steps(ci_)

                for st_ in prep1a(0):
                    st_()
                wld = 0
                it = 0
                for ci in range(NCH):
                    own = ci >= 4
                    pr = ci % 2
                    hT = hT_b[pr]
                    hk = ("hT", pr)
                    cosk, sink, cosq, sinq = cosk_b[pr], sink_b[pr], cosq_b[pr], sinq_b[pr]
                    nxt = prep1a(ci + 1) if ci + 1 < NCH else []
                    groups = [0, 1, 2, 3, 4, 5, 6, 7] if own else [3, 4, 5, 6, 7]
                    npop = -(-len(nxt) // (3 * len(groups) - 2))
                    for g3 in groups:
                        w = wq[wld % 2]
                        wkey = ("wq", wld % 2)
                        wld += 1
                        nj = min(3, 22 - g3 * 3)
                        for j in range(nj):
                            p.dma("pool", w[:, j, :, :], w_in_in[g3 * 3 + j].rearrange("p (k m) -> p k m", k=16), writes=[(wkey, j)])
                        for j in range(nj):
                            for _ in range(npop):
                                if nxt:
                                    nxt.pop(0)()
                            mt = g3 * 3 + j
                            bank = 2 + (it % 2)
                            it += 1
                            for kt in range(16):
                                p.op("pe", lambda w=w, j=j, kt=kt, bank=bank: nc.tensor.matmul(
                                    ps[bank][:], w[:, j, kt, :], hT[:, kt, :], start=(kt == 0), stop=(kt == 15)),
                                    reads=[(wkey, j), (hk, kt)], writes=[PS(bank)] if kt == 0 else [], sig=(kt == 15))
                            p.lastw[PS(bank)] = ("s_pe", p.cnt["pe"], "pe")
                            csl = slice(ci * NT, (ci + 1) * NT)
                            if g3 == 4:
                                p.op("act", lambda bank=bank, j=j, csl=csl: nc.scalar.copy(vT[:, j, csl], ps[bank][:]),
                                     reads=[PS(bank)], writes=[("vT", j, ci)])
                                continue
                            if g3 >= 5:
                                o_ = mt - 15
                                xb_ = xbf[it % 2]
                                kx = ("xbf", it % 2)
                                if own:
                                    p.op("act", lambda bank=bank, xb_=xb_: nc.scalar.copy(xb_[:], ps[bank][:]), reads=[PS(bank)], writes=[kx])
                                else:
                                    p.op("act", lambda bank=bank, xb_=xb_: nc.scalar.activation(xb_[:], ps[bank][:], AF.Identity, scale=cc[:, FLAG:FLAG + 1]),
                                         reads=[PS(bank), "cc"], writes=[kx])
                                p.dma("sp", uT_scr[o_, :, csl], xb_[:], reads=[kx], writes=[("uscr", o_, ci)])
                                continue
                            isq = g3 < 3
                            ctab, stab = (cosq, sinq) if isq else (cosk, sink)
                            ck, sk = (("cosq", pr), ("sinq", pr)) if isq else (("cosk", pr), ("sink", pr))
                            b2 = 4 + (it % 2)
                            xb_ = xbf[it % 2]
                            a1 = t1[it % 2]
                            a2 = t2[it % 2]
                            kx, k1, k2 = ("xbf", it % 2), ("t1", it % 2), ("t2", it % 2)
                            p.op("act", lambda bank=bank, xb_=xb_: nc.scalar.copy(xb_[:], ps[bank][:]),
                                 reads=[PS(bank)], writes=[kx])
                            p.op("pe", lambda b2=b2, xb_=xb_: nc.tensor.matmul(ps[b2][:], rmat_b[:], xb_[:], start=True, stop=True),
                                 reads=["rmat_b", kx], writes=[PS(b2)])
                            p.op("dve", lambda bank=bank, a1=a1, ctab=ctab: nc.vector.tensor_tensor(a1[:], ps[bank][:], ctab[:], ALU.mult),
                                 reads=[PS(bank), ck], writes=[k1])
                            p.op("dve", lambda b2=b2, a2=a2, stab=stab: nc.vector.tensor_tensor(a2[:], ps[b2][:], stab[:], ALU.mult),
                                 reads=[PS(b2), sk], writes=[k2])
                            if isq:
                                dst = qT[:, mt, (ci - 4) * NT:(ci - 3) * NT]
                                dk = ("qT", mt, ci)
                            else:
                                dst = kT[:, j, csl]
                                dk = ("kT", j, ci)
                            p.op("pool", lambda dst=dst, a1=a1, a2=a2: nc.gpsimd.tensor_tensor(dst, a1[:], a2[:], ALU.add),
                                 reads=[k1, k2], writes=[dk])
                    while nxt:
                        nxt.pop(0)()
                if "qkv" in debug:
                    for j in range(9):
                        p.dma("sp", dout("qT%d" % j, [128, 2048], BF16)[:, :], qT[:, j, :], reads=[("qT", j, c) for c in range(4, 8)], writes=["o_q%d" % j])
                    for j in range(3):
                        p.dma("sp", dout("kT%d" % j, [128, 4096], BF16)[:, :], kT[:, j, :], reads=[("kT", j, c) for c in range(8)], writes=["o_k%d" % j])
                        p.dma("sp", dout("vT%d" % j, [128, 4096], BF16)[:, :], vT[:, j, :], reads=[("vT", j, c) for c in range(8)], writes=["o_v%d" % j])
                p.barrier()
                ckpt("p1a")

            with contextlib.ExitStack() as ph:
                vblk = T(ph, "vblk", [128, 32, 384], BF16)
                numacc = T(ph, "numacc", [128, 3, 2048], F32)
                denacc = T(ph, "denacc", [128, 3, 2048], F32)
                PT = [T(ph, "PT%d" % i, [128, 2, 256], BF16) for i in range(2)]
                attn = T(ph, "attn", [128, 3, NT], F32)
                sqa = T(ph, "sqa", [128, NT], BF16)
                rtmp = T(ph, "rtmp", [128, NT], F32)
                rrep = T(ph, "rrep", [128, NT], F32)
                atb = T(ph, "atb", [128, 3, NT], BF16)
                it = 0
                first = True
                for gi, d in ((2, 16), (1, 4), (0, 1)):
                    NB = 2048 // (128 * d)
                    nblk = d * (NB + 1)
                    for r in range(d):
                        for bb in range(NB + 1):
                            bi = r * (NB + 1) + bb
                            base = 2048 - 128 * d + r + 128 * d * bb
                            tb = 6 + (bi % 2)
                            for pt in range(3):
                                p.op("pe", lambda pt=pt, tb=tb, base=base, d=d: nc.tensor.transpose(
                                    psb[tb][:, pt * 128:(pt + 1) * 128], vT[:, pt, base:base + 127 * d + 1:d], ident_b[:]),
                                    reads=["ident_b"], writes=[PS(tb)] if pt == 0 else [], sig=(pt == 2))
                            p.lastw[PS(tb)] = ("s_pe", p.cnt["pe"], "pe")
                            p.op("dve", lambda bi=bi, tb=tb: nc.vector.tensor_copy(vblk[:, bi, :], psb[tb][:, 0:384]),
                                 reads=[PS(tb)], writes=[("vblk", bi)])
                    for r in range(d):
                        for b in range(NB):
                            qsl = slice(r + d * 128 * b, r + d * 128 * b + 127 * d + 1, d)
                            kcur = 2048 + r + d * 128 * b
                            kprev = kcur - 128 * d
                            bi_prev = r * (NB + 1) + b
                            bi_cur = bi_prev + 1
                            mprev = 2 if b == 0 else 1
                            for pt in range(3):
                                pbuf = PT[it % 2]
                                pk = ("PT", it % 2)
                                sb = [(it % 2) * 2, (it % 2) * 2 + 1]
                                ndb = 4 + (it % 2)
                                it += 1
                                for hp in range(2):
                                    rows = slice(64 * hp, 64 * hp + 64)
                                    for which, (kbase, mi) in enumerate(((kprev, mprev), (kcur, 0))):
                                        osl = slice(which * 128, which * 128 + 128)
                                        p.op("pe", lambda hp=hp, rows=rows, kbase=kbase, osl=osl, pt=pt, qsl=qsl, gi=gi, sb=sb, d=d: nc.tensor.matmul(
                                            ps[sb[hp]][:, osl], kT[rows, pt, kbase:kbase + 127 * d + 1:d], qT[rows, gi * 3 + pt, qsl],
                                            start=True, stop=True),
                                            reads=[], writes=[PS(sb[hp])] if which == 0 else [], sig=(which == 1))
                                    p.lastw[PS(sb[hp])] = ("s_pe", p.cnt["pe"], "pe")
                                    p.op("act", lambda hp=hp, pbuf=pbuf, sb=sb: nc.scalar.activation(pbuf[:, hp, :], ps[sb[hp]][:, 0:256], AF.Exp),
                                         reads=[PS(sb[hp])], writes=[(pk, hp)])
                                    moff = 256 if b == 0 else 0
                                    p.op("dve", lambda hp=hp, pbuf=pbuf, moff=moff: nc.vector.tensor_tensor(
                                        pbuf[:, hp, :], pbuf[:, hp, :], masks_b[:, moff:moff + 256], ALU.mult),
                                        reads=[(pk, hp), "masks_b"], writes=[(pk, hp)])
                                for hp in range(2):
                                    rows = slice(64 * hp, 64 * hp + 64)
                                    hcol = (2 * pt + hp) * 64
                                    for which, bi in enumerate((bi_prev, bi_cur)):
                                        p.op("pe", lambda rows=rows, hcol=hcol, bi=bi, which=which, pbuf=pbuf, hp=hp, ndb=ndb: nc.tensor.matmul(
                                            ps[ndb][rows, 0:128], vblk[:, bi, hcol:hcol + 64], pbuf[:, hp, which * 128:(which + 1) * 128],
                                            start=(which == 0), stop=(which == 1)),
                                            reads=[("vblk", bi), (pk, hp)], writes=[PS(ndb)] if (hp == 0 and which == 0) else [], sig=False)
                                    for which in range(2):
                                        p.op("pe", lambda rows=rows, which=which, pbuf=pbuf, hp=hp, ndb=ndb: nc.tensor.matmul(
                                            ps[ndb][rows, 128:256], ones_b[:, 0:64], pbuf[:, hp, which * 128:(which + 1) * 128],
                                            start=(which == 0), stop=(which == 1)),
                                            reads=["ones_b", (pk, hp)], writes=[], sig=(hp == 1 and which == 1))
                                p.lastw[PS(ndb)] = ("s_pe", p.cnt["pe"], "pe")
                                if first:
                                    p.op("dve", lambda pt=pt, qsl=qsl, ndb=ndb: nc.vector.tensor_copy(numacc[:, pt, qsl], ps[ndb][:, 0:128]),
                                         reads=[PS(ndb)], writes=[("num", pt)])
                                    p.op("dve", lambda pt=pt, qsl=qsl, ndb=ndb: nc.vector.tensor_copy(denacc[:, pt, qsl], ps[ndb][:, 128:256]),
                                         reads=[PS(ndb)], writes=[("den", pt)])
                                else:
                                    p.op("dve", lambda pt=pt, qsl=qsl, ndb=ndb: nc.vector.tensor_tensor(numacc[:, pt, qsl], ps[ndb][:, 0:128], numacc[:, pt, qsl], ALU.add),
                                         reads=[PS(ndb), ("num", pt)], writes=[("num", pt)])
                                    p.op("dve", lambda pt=pt, qsl=qsl, ndb=ndb: nc.vector.tensor_tensor(denacc[:, pt, qsl], ps[ndb][:, 128:256], denacc[:, pt, qsl], ALU.add),
                                         reads=[PS(ndb), ("den", pt)], writes=[("den", pt)])
                    first = False
                for c4 in range(4):
                    csl = slice(c4 * NT, (c4 + 1) * NT)
                    for pt in range(3):
                        p.op("act", lambda pt=pt, csl=csl: nc.scalar.activation(rtmp[:], denacc[:, pt, csl], AF.Ln),
                             reads=[("den", pt)], writes=["rtmp"])
                        p.op("act", lambda: nc.scalar.activation(rtmp[:], rtmp[:], AF.Exp, scale=-1.0),
                             reads=["rtmp"], writes=["rtmp"])
                        p.op("dve", lambda pt=pt, csl=csl: nc.vector.tensor_tensor(attn[:, pt, :], numacc[:, pt, csl], rtmp[:], ALU.mult),
                             reads=[("num", pt), "rtmp"], writes=[("attn", pt)])
                        p.op("act", lambda pt=pt: nc.scalar.activation(sqa[:], attn[:, pt, :], AF.Square),
                             reads=[("attn", pt)], writes=["sqa"])
                        p.op("pe", lambda pt=pt: nc.tensor.matmul(ps[6][:], ones_b[:], sqa[:], start=(pt == 0), stop=(pt == 2)),
                             reads=["ones_b", "sqa"], writes=[PS(6)] if pt == 0 else [])
                    p.lastw[PS(6)] = ("s_pe", p.cnt["pe"], "pe")
                    rstd_rep(rrep, ps[6][:], 384.0, PS(6), "rrep", rtmp, "rtmp")
                    for pt in range(3):
                        p.op("dve", lambda pt=pt: nc.vector.scalar_tensor_tensor(
                            atb[:, pt, :], attn[:, pt, :], gcols[:, G_ATT + pt:G_ATT + pt + 1], rrep[:], ALU.mult, ALU.mult),
                            reads=[("attn", pt), "gcols", "rrep"], writes=[("atb", pt)])
                        p.dma("sp", catT[pt, :, csl], atb[:, pt, :], reads=[("atb", pt)], writes=[("cat", pt, c4)])
                if "att" in debug:
                    p.barrier()
                p.barrier()
        if "att" in debug:
            with contextlib.ExitStack() as ph:
                tmpb = T(ph, "tmpb", [128, 2048], BF16)
                for pt in range(3):
                    p.dma("sp", tmpb[:], catT[pt, :, :], writes=["tmpb"])
                    p.dma("sp", dout("att%d" % pt, [128, 2048], BF16)[:, :], tmpb[:], reads=["tmpb"], writes=["o_att%d" % pt])
                p.barrier()
        ckpt("p2")
        if True:
            if True:
                pass

        with contextlib.ExitStack() as pb:
            LB = T(pb, "LB", [128, 7, 4, 128], BF16)
            LBs = T(pb, "LBs", [128, 7, 4, 128], BF16)
            L1 = T(pb, "L1", [128, 56, 64], BF16)
            L2 = T(pb, "L2", [128, 56, 64], BF16)
            lsc = T(pb, "lsc", [128, 12, 56], F32)
            state = T(pb, "state", [128, 56], F32)
            wglu = T(pb, "wglu", [128, 7, 7, 128], BF16)
            ABSA, FT, PH0 = 0, 1, 2
            for j in range(7):
                p.dma("pool", wglu[:, j, :, :], w_glu_in[j].rearrange("p (k m) -> p k m", k=7), writes=[("wglu", j)])
            with contextlib.ExitStack() as ph:
                al = T(ph, "al", [128, 3, 1792], F32)
                bx = T(ph, "bx", [128, 2, 1792], F32)
                ln_ = T(ph, "ln_", [128, 3, 56], F32)
                V = [T(ph, "vk%d" % i, [128, 56], F32) for i in range(6)]
                W = [T(ph, "wk%d" % i, [128, 1792], F32) for i in range(10)]
                p.dma("sp", al[:], alay_in.rearrange("p (a n) -> p a n", a=3), writes=["al"])
                p.dma("sp", bx[:], bexp_in.rearrange("p (a n) -> p a n", a=2), writes=["bx"])
                p.dma("sp", ln_[:], lanes_in.rearrange("p (a n) -> p a n", a=3), writes=["ln"])
                kk = [0]

                def ew(eng, fn, r, w_):
                    p.op(eng, fn, reads=r, writes=w_)

                def abar(are, aim, ldt, dt_, er, frac, co, si, tmp, tmp2, tag):
                    ew("act", lambda: nc.scalar.activation(dt_, ldt, AF.Exp), ["al", "ln"], [tag + "dt"])
                    ew("dve", lambda: nc.vector.tensor_tensor(tmp, are, dt_, ALU.mult), ["al", "ln", tag + "dt"], [tag + "tmp"])
                    ew("act", lambda: nc.scalar.activation(er, tmp, AF.Exp), [tag + "tmp"], [tag + "er"])
                    ew("dve", lambda: nc.vector.scalar_tensor_tensor(tmp, aim, 1.0 / (2 * PI), dt_, ALU.mult, ALU.mult), ["al", "ln", tag + "dt", tag + "er"], [tag + "tmp"])
                    ew("dve", lambda: nc.vector.tensor_scalar(tmp2, tmp, MAGIC, None, ALU.add), [tag + "tmp"], [tag + "tmp2"])
                    ew("dve", lambda: nc.vector.scalar_tensor_tensor(frac, tmp2, MAGIC, tmp, ALU.subtract, ALU.subtract), [tag + "tmp2", tag + "tmp"], [tag + "frac"])
                    ew("act", lambda: nc.scalar.activation(si, frac, AF.Sin, scale=-2.0 * PI), [tag + "frac"], [tag + "si"])
                    ew("act", lambda: nc.scalar.activation(tmp2, frac, AF.Abs), [tag + "frac", tag + "tmp2"], [tag + "tmp2"])
                    ew("act", lambda: nc.scalar.activation(co, tmp2, AF.Sin, scale=-2.0 * PI, bias=cc[:, 5:6]), [tag + "tmp2", "cc"], [tag + "co"])

                are, aim, ldt = al[:, 0, :], al[:, 1, :], al[:, 2, :]
                dt_, er, nfr, co, si, tmp, tmp2 = (W[i][:] for i in range(7))
                abar(are, aim, ldt, dt_, er, nfr, co, si, tmp, tmp2, "L")
                abr, abi, den = W[7][:], W[8][:], W[9][:]
                ew("dve", lambda: nc.vector.tensor_tensor(abr, er, co, ALU.mult), ["Ler", "Lco"], ["abr"])
                ew("dve", lambda: nc.vector.tensor_tensor(abi, er, si, ALU.mult), ["Ler", "Lsi"], ["abi"])
                ew("dve", lambda: nc.vector.tensor_scalar(abr, abr, -1.0, None, ALU.add), ["abr"], ["abr"])
                ew("dve", lambda: nc.vector.tensor_tensor(den, are, are, ALU.mult), ["al"], ["den"])
                ew("dve", lambda: nc.vector.tensor_tensor(tmp, aim, aim, ALU.mult), ["al", "Lco", "Ltmp"], ["Ltmp"])
                ew("dve", lambda: nc.vector.tensor_tensor(den, den, tmp, ALU.add), ["den", "Ltmp"], ["den"])
                ew("act", lambda: nc.scalar.activation(den, den, AF.Ln), ["den"], ["den"])
                ew("act", lambda: nc.scalar.activation(den, den, AF.Exp, scale=-1.0), ["den"], ["den"])
                nre, nim = co, si
                ew("dve", lambda: nc.vector.tensor_tensor(tmp, abr, are, ALU.mult), ["abr", "al", "Ltmp"], ["Ltmp"])
                ew("dve", lambda: nc.vector.tensor_tensor(tmp2, abi, aim, ALU.mult), ["abi", "al", "Ltmp2", "Lco"], ["Ltmp2"])
                ew("dve", lambda: nc.vector.tensor_tensor(nre, tmp, tmp2, ALU.add), ["Ltmp", "Ltmp2", "abr", "Lco"], ["nre"])
                ew("dve", lambda: nc.vector.tensor_tensor(tmp, abi, are, ALU.mult), ["abi", "al", "nre"], ["Ltmp"])
                ew("dve", lambda: nc.vector.tensor_tensor(tmp2, abr, aim, ALU.mult), ["abr", "al", "nre"], ["Ltmp2"])
                ew("dve", lambda: nc.vector.tensor_tensor(nim, tmp, tmp2, ALU.subtract), ["Ltmp", "Ltmp2", "abi", "Lsi"], ["nim"])
                ew("dve", lambda: nc.vector.tensor_tensor(nre, nre, den, ALU.mult), ["nre", "den"], ["nre"])
                ew("dve", lambda: nc.vector.tensor_tensor(nim, nim, den, ALU.mult), ["nim", "den"], ["nim"])
                bre, bim = bx[:, 0, :], bx[:, 1, :]
                bbr, bbi = W[7][:], W[8][:]
                ew("dve", lambda: nc.vector.tensor_tensor(tmp, nre, bre, ALU.mult), ["nre", "bx", "nim"], ["Ltmp"])
                ew("dve", lambda: nc.vector.tensor_tensor(tmp2, nim, bim, ALU.mult), ["nim", "bx", "nre"], ["Ltmp2"])
                ew("dve", lambda: nc.vector.tensor_tensor(bbr, tmp, tmp2, ALU.subtract), ["Ltmp", "Ltmp2", "abr", "nre", "nim"], ["bbr"])
                ew("dve", lambda: nc.vector.tensor_tensor(tmp, nre, bim, ALU.mult), ["nre", "bx", "bbr"], ["Ltmp"])
                ew("dve", lambda: nc.vector.tensor_tensor(tmp2, nim, bre, ALU.mult), ["nim", "bx", "bbr"], ["Ltmp2"])
                ew("dve", lambda: nc.vector.tensor_tensor(bbi, tmp, tmp2, ALU.add), ["Ltmp", "Ltmp2", "abi", "nim"], ["bbi"])
                v4 = lambda a: a.rearrange("p (o g n) -> p o g n", o=7, g=4)
                ew("dve", lambda: nc.vector.tensor_copy(LB[:, :, :, 0:64], v4(bbr)), ["bbr"], ["LB"])
                ew("dve", lambda: nc.vector.tensor_copy(LB[:, :, :, 64:128], v4(bbi)), ["bbi"], ["LB"])
                ew("dve", lambda: nc.vector.tensor_copy(LBs[:, :, :, 0:64], v4(bbi)), ["bbi"], ["LBs"])
                ew("dve", lambda: nc.vector.tensor_scalar(LBs[:, :, :, 64:128], v4(bbr), -1.0, None, ALU.mult), ["bbr"], ["LBs"])
                p.barrier()
            with contextlib.ExitStack() as ph:
                cx = T(ph, "cx", [128, 2, 3584], F32)
                ln_ = T(ph, "ln_b", [128, 3, 56], F32)
                V = [T(ph, "vkb%d" % i, [128, 56], F32) for i in range(6)]
                p.dma("sp", cx[:], cexp_in.rearrange("p (a n) -> p a n", a=2), writes=["cx"])
                p.dma("sp", ln_[:], lanes_in.rearrange("p (a n) -> p a n", a=3), writes=["ln"])
                c3 = lambda a: a.rearrange("p (g c) -> p g c", g=56)
                ew("dve", lambda: nc.vector.tensor_copy(L1[0:64, :, :], c3(cx[0:64, 0, :])), ["cx"], ["L1"])
                ew("dve", lambda: nc.vector.tensor_scalar(L1[64:128, :, :], c3(cx[64:128, 1, :]), -1.0, None, ALU.mult), ["cx"], ["L1"])
                ew("dve", lambda: nc.vector.tensor_scalar(L2[0:64, :, :], c3(cx[0:64, 1, :]), -1.0, None, ALU.mult), ["cx"], ["L2"])
                ew("dve", lambda: nc.vector.tensor_scalar(L2[64:128, :, :], c3(cx[64:128, 0, :]), -1.0, None, ALU.mult), ["cx"], ["L2"])
                dt2, er2, nfr2, co2, si2, tm2 = (V[i][:] for i in range(6))
                abar(ln_[:, 0, :], ln_[:, 1, :], ln_[:, 2, :], dt2, er2, nfr2, co2, si2, tm2, lsc[:, 11, :], "V")
                ew("dve", lambda: nc.vector.tensor_copy(lsc[:, ABSA, :], er2), ["Ver"], ["lsc"])
                ew("dve", lambda: nc.vector.tensor_scalar(lsc[:, FT, :], nfr2, -1.0, None, ALU.mult), ["Vfrac"], ["lsc"])
                ew("dve", lambda: nc.vector.tensor_scalar(tm2, lsc[:, FT, :], 512.0, MAGIC, ALU.mult, ALU.add), ["lsc", "Vtmp", "Vco"], ["Vtmp"])
                ew("dve", lambda: nc.vector.tensor_scalar(tm2, tm2, MAGIC, None, ALU.subtract), ["Vtmp"], ["Vtmp"])
                ew("dve", lambda: nc.vector.scalar_tensor_tensor(dt2, lsc[:, FT, :], 512.0, tm2, ALU.mult, ALU.subtract), ["lsc", "Vtmp", "Ver", "Vdt"], ["g512"])
                for c in range(8):
                    ew("dve", lambda c=c: nc.vector.tensor_scalar(tm2, dt2, float(c), MAGIC, ALU.mult, ALU.add), ["g512", "Vtmp", "lsc"], ["Vtmp"])
                    ew("dve", lambda: nc.vector.tensor_scalar(tm2, tm2, MAGIC, None, ALU.subtract), ["Vtmp"], ["Vtmp"])
                    ew("dve", lambda c=c: nc.vector.scalar_tensor_tensor(lsc[:, PH0 + c, :], dt2, float(c), tm2, ALU.mult, ALU.subtract), ["g512", "Vtmp"], ["lsc"])
                ew("dve", lambda: nc.vector.memset(state[:], 0.0), [], ["state"])
                if "ssmsetup" in debug:
                    for nm, t_, shp in (("LB", LB, [128, 7 * 4 * 128]), ("LBs", LBs, [128, 3584]), ("L1", L1, [128, 3584]), ("L2", L2, [128, 3584])):
                        p.dma("sp", dout(nm, shp, BF16)[:, :], t_[:].rearrange("p a b c -> p (a b c)") if nm in ("LB", "LBs") else t_[:].rearrange("p a b -> p (a b)"), reads=[nm], writes=["o_" + nm])
                    p.dma("sp", dout("lsc", [128, 12 * 56])[:, :], lsc[:].rearrange("p a b -> p (a b)"), reads=["lsc"], writes=["o_lsc"])
                p.barrier()
                ckpt("p1b0")

            with contextlib.ExitStack() as ph:
                uT = T(ph, "uT", [128, 7, NT], BF16)
                NB2 = 2
                tu = [T(ph, "tu%d" % i, [128, NT], F32) for i in range(3)]
                tv = [T(ph, "tv%d" % i, [128, NT], F32) for i in range(3)]
                tS = [T(ph, "tS%d" % i, [128, NT], F32) for i in range(3)]
                tC = [T(ph, "tC%d" % i, [128, NT], F32) for i in range(3)]
                ta = [T(ph, "ta%d" % i, [128, NT], F32) for i in range(NB2)]
                tb_ = [T(ph, "tb%d" % i, [128, NT], F32) for i in range(NB2)]
                tw = [T(ph, "tw%d" % i, [128, NT], F32) for i in range(NB2)]
                M1 = [T(ph, "M1_%d" % i, [128, NT], BF16) for i in range(NB2)]
                M2 = [T(ph, "M2_%d" % i, [128, NT], BF16) for i in range(NB2)]
                ygb = T(ph, "ygb", [128, 7, NT], BF16)
                ssm = T(ph, "ssm", [128, 7, NT], F32)
                yv = [T(ph, "yv%d" % i, [128, NT], F32) for i in range(2)]
                sg = [T(ph, "sg%d" % i, [128, NT], F32) for i in range(2)]
                sqs = [T(ph, "sqs%d" % i, [128, NT], BF16) for i in range(2)]
                rtmp = T(ph, "rtmp2", [128, NT], F32)
                rrep = T(ph, "rrep2", [128, NT], F32)
                uTs = [uT, T(ph, "uT1", [128, 7, NT], BF16)]

                def load_u(ci_):
                    p.dma("sp", uTs[ci_ % 2][:], uT_scr[:, :, ci_ * NT:(ci_ + 1) * NT].rearrange("o p n -> p o n"),
                          writes=[("uT", ci_ % 2, o) for o in range(7)])

                load_u(0)
                for ci in range(NCH):
                    own = ci >= 4
                    par = ci % 2
                    uT = uTs[par]
                    nxt = []
                    if ci + 1 < NCH:
                        load_u(ci + 1)
                    items = [(o, hb, gq) for o in range(7) for hb in range(2) for gq in range(4)]

                    def stageAmm(idx):
                        o, hb, gq = items[idx]
                        rows = slice(64 * hb, 64 * hb + 64)
                        s2 = idx % 2
                        bA, bB = 3 + 2 * s2, 4 + 2 * s2
                        p.op("pe", lambda: nc.tensor.matmul(ps[bA][:], LB[rows, o, gq, :], uT[rows, o, :], start=True, stop=True),
                             reads=["LB", ("uT", par, o)], writes=[PS(bA)])
                        p.op("pe", lambda: nc.tensor.matmul(ps[bB][:], LBs[rows, o, gq, :], uT[rows, o, :], start=True, stop=True),
                             reads=["LBs", ("uT", par, o)], writes=[PS(bB)])

                    def stageAu(idx):
                        o, hb, gq = items[idx]
                        g = 8 * o + 4 * hb + gq
                        s3 = idx % 3
                        p.op("pool", lambda: nc.gpsimd.tensor_scalar(
                            tu[s3][:], jidx[:], lsc[:, FT, g:g + 1], lsc[:, PH0 + ci, g:g + 1], ALU.mult, ALU.add),
                            reads=["jidx", "lsc"], writes=[("tu", s3)])

                    def stageA(idx):
                        s3 = idx % 3
                        p.op("dve", lambda: nc.vector.tensor_scalar(tv[s3][:], tu[s3][:], MAGIC, None, ALU.add),
                             reads=[("tu", s3)], writes=[("tv", s3)])
                        p.op("dve", lambda: nc.vector.scalar_tensor_tensor(tv[s3][:], tv[s3][:], MAGIC, tu[s3][:], ALU.subtract, ALU.subtract),
                             reads=[("tv", s3), ("tu", s3)], writes=[("tv", s3)])
                        p.op("act", lambda: nc.scalar.activation(tS[s3][:], tv[s3][:], AF.Sin, scale=-2.0 * PI),
                             reads=[("tv", s3)], writes=[("tS", s3)])
                        p.op("act", lambda: nc.scalar.activation(tu[s3][:], tv[s3][:], AF.Abs),
                             reads=[("tv", s3)], writes=[("tu", s3)])
                        p.op("act", lambda: nc.scalar.activation(tC[s3][:], tu[s3][:], AF.Sin, scale=-2.0 * PI, bias=cc[:, 5:6]),
                             reads=[("tu", s3), "cc"], writes=[("tC", s3)])

                    def stageB(idx):
                        o, hb, gq = items[idx]
                        g = 8 * o + 4 * hb + gq
                        rows = slice(64 * hb, 64 * hb + 64)
                        s3 = idx % 3
                        s_ = idx % 2
                        bA, bB = 3 + 2 * s_, 4 + 2 * s_
                        p.op("dve", lambda: nc.vector.tensor_tensor(ta[s_][:], ps[bA][:], tC[s3][:], ALU.mult),
                             reads=[PS(bA), ("tC", s3)], writes=[("ta", s_)])
                        p.op("dve", lambda: nc.vector.tensor_tensor(tb_[s_][:], ps[bB][:], tS[s3][:], ALU.mult),
                             reads=[PS(bB), ("tS", s3)], writes=[("tb", s_)])
                        p.op("dve", lambda: nc.vector.tensor_tensor(ta[s_][:], ta[s_][:], tb_[s_][:], ALU.add),
                             reads=[("ta", s_), ("tb", s_)], writes=[("ta", s_)])
                        p.op("dve", lambda: nc.vector.tensor_tensor_scan(
                            tw[s_][:], lsc[:, ABSA, g:g + 1].to_broadcast([128, NT]), ta[s_][:], state[:, g:g + 1], ALU.mult, ALU.add),
                            reads=[("ta", s_), "lsc", ("state", g)], writes=[("tw", s_)])
                        p.op("pool", lambda: nc.gpsimd.tensor_copy(state[:, g:g + 1], tw[s_][:, NT - 1:NT]),
                             reads=[("tw", s_)], writes=[("state", g)])
                        if own:
                            p.op("dve", lambda: nc.vector.tensor_tensor(M1[s_][:], tC[s3][:], tw[s_][:], ALU.mult),
                                 reads=[("tC", s3), ("tw", s_)], writes=[("M1", s_)])
                            p.op("dve", lambda: nc.vector.tensor_tensor(M2[s_][:], tS[s3][:], tw[s_][:], ALU.mult),
                                 reads=[("tS", s3), ("tw", s_)], writes=[("M2", s_)])
                            yk = ("ps", 7, hb)
                            p.op("pe", lambda: nc.tensor.matmul(ps[7][rows, :], L1[:, g, :], M1[s_][:], start=(gq == 0), stop=False),
                                 reads=["L1", ("M1", s_)], writes=[yk] if gq == 0 else [], sig=False)
                            p.op("pe", lambda: nc.tensor.matmul(ps[7][rows, :], L2[:, g, :], M2[s_][:], start=False, stop=(gq == 3)),
                                 reads=["L2", ("M2", s_)], writes=[], sig=True)
                            if gq == 3:
                                p.lastw[yk] = ("s_pe", p.cnt["pe"], "pe")
                            if hb == 1 and gq == 3:
                                y_ = yv[o % 2]
                                p.op("dve", lambda: nc.vector.scalar_tensor_tensor(
                                    y_[:], uT[:, o, :], gcols[:, DSK + o:DSK + o + 1], ps[7][:], ALU.mult, ALU.add),
                                    reads=[("uT", par, o), "gcols", ("ps", 7, 0), ("ps", 7, 1)], writes=[("yv", o % 2)])
                                p.op("act", lambda: nc.scalar.activation(ygb[:, o, :], y_[:], AF.Gelu_apprx_tanh),
                                     reads=[("yv", o % 2)], writes=[("ygb", o)])

                    stageAu(0)
                    stageAu(1)
                    stageAu(2)
                    stageA(0)
                    stageA(1)
                    stageAmm(0)
                    for idx in range(len(items)):
                        if nxt and (idx % 3 != 2):
                            nxt.pop(0)()
                        if idx + 3 < len(items):
                            stageAu(idx + 3)
                        if idx + 2 < len(items):
                            stageA(idx + 2)
                        if idx + 1 < len(items):
                            stageAmm(idx + 1)
                        stageB(idx)
                        if idx % 4 == 3:
                            conv_step()
                    while nxt:
                        nxt.pop(0)()
                    if own:
                        c4 = ci - 4
                        csl = slice(c4 * NT, (c4 + 1) * NT)
                        for mt in range(7):
                            for kt in range(7):
                                p.op("pe", lambda mt=mt, kt=kt: nc.tensor.matmul(ps[2][:], wglu[:, mt, kt, :], ygb[:, kt, :], start=(kt == 0), stop=(kt == 6)),
                                     reads=[("wglu", mt), ("ygb", kt)], writes=[PS(2)] if kt == 0 else [], sig=(kt == 6))
                            p.lastw[PS(2)] = ("s_pe", p.cnt["pe"], "pe")
                            s2 = sg[mt % 2]
                            q2 = sqs[mt % 2]
                            p.op("act", lambda mt=mt, s2=s2: nc.scalar.activation(s2[:], ps[2][:], AF.Sigmoid, bias=gcols[:, B_GLU + mt:B_GLU + mt + 1]),
                                 reads=[PS(2), "gcols"], writes=[("sg", mt % 2)])
                            p.op("dve", lambda mt=mt, s2=s2: nc.vector.tensor_tensor(ssm[:, mt, :], ygb[:, mt, :], s2[:], ALU.mult),
                                 reads=[("ygb", mt), ("sg", mt % 2)], writes=[("ssm", mt)])
                            p.op("act", lambda mt=mt, q2=q2: nc.scalar.activation(q2[:], ssm[:, mt, :], AF.Square),
                                 reads=[("ssm", mt)], writes=[("sqs", mt % 2)])
                            p.op("pe", lambda mt=mt, q2=q2: nc.tensor.matmul(ps[0][:], ones_b[:], q2[:], start=(mt == 0), stop=(mt == 6)),
                                 reads=["ones_b", ("sqs", mt % 2)], writes=[PS(0)] if mt == 0 else [])
                        p.lastw[PS(0)] = ("s_pe", p.cnt["pe"], "pe")
                        rstd_rep(rrep, ps[0][:], 896.0, PS(0), "rrep2", rtmp, "rtmp2")
                        for mt in range(7):
                            p.op("dve", lambda mt=mt: nc.vector.scalar_tensor_tensor(
                                ygb[:, mt, :], ssm[:, mt, :], gcols[:, G_SSM + mt:G_SSM + mt + 1], rrep[:], ALU.mult, ALU.mult),
                                reads=[("ssm", mt), "gcols", "rrep2"], writes=[("ygb", mt)])
                            p.dma("sp", catT[3 + mt, :, csl], ygb[:, mt, :], reads=[("ygb", mt)], writes=[("cat", 3 + mt, c4)])
                p.barrier()
        if "ssm" in debug:
            with contextlib.ExitStack() as ph:
                tmpb = T(ph, "tmpb2", [128, 2048], BF16)
                for mt in range(7):
                    p.dma("sp", tmpb[:], catT[3 + mt, :, :], writes=["tmpb"])
                    p.dma("sp", dout("ssm%d" % mt, [128, 2048], BF16)[:, :], tmpb[:], reads=["tmpb"], writes=["o_ssm%d" % mt])
                p.barrier()
        ckpt("p1b")

        with contextlib.ExitStack() as ph:
            xT = T(ph, "xT", [128, 16, NT], F32)
            xt = [T(ph, "xt3_%d" % i, [128, D], F32) for i in range(2)]
            yT = T(ph, "yT", [128, 16, NT], F32)
            yTb = yT[:].rearrange("p a b -> p (a b)").bitcast(BF16)
            h2T = yTb[:, 0:8192].rearrange("p (a b) -> p a b", a=16)
            catc = yTb[:, 8192:8192 + 5120].rearrange("p (a b) -> p a b", a=10)
            aT = T(ph, "aT", [128, 64, NT], BF16)
            mixT = aT[:].rearrange("p a b -> p (a b)").bitcast(F32)[:, 0:8192].rearrange("p (a b) -> p a b", a=16)
            wb = [T(ph, "wb%d" % i, [128, 8192], BF16) for i in range(2)]
            rtmp = T(ph, "rtmp3", [128, NT], F32)
            rrep = T(ph, "rrep3", [128, NT], F32)
            sq3 = [T(ph, "sq3_%d" % i, [128, NT], BF16) for i in range(2)]
            tm3 = [T(ph, "tm3_%d" % i, [128, NT], F32) for i in range(2)]
            rl3 = [T(ph, "rl3_%d" % i, [128, NT], BF16) for i in range(2)]
            ALL_H2 = [("h2T", f) for f in range(16)] + [("catc", k) for k in range(10)]
            ALL_A = [("aT", j) for j in range(64)]
            wl = 0
            for c4 in range(4):
                csl = slice(c4 * NT, (c4 + 1) * NT)
                for tt in range(4):
                    row0 = c4 * NT + tt * 128
                    xb_ = xt[tt % 2]
                    p.dma("sp", xb_[:], x_own[row0:row0 + 128, :], writes=[("xt", tt % 2)])
                    for f4 in range(4):
                        bank = f4 % 2
                        for i in range(4):
                            f = f4 * 4 + i
                            p.op("pe", lambda xb_=xb_, f=f, i=i, bank=bank: nc.tensor.transpose(
                                ps[bank][:, i * 128:(i + 1) * 128], xb_[:, f * 128:(f + 1) * 128], ident_f[:]),
                                reads=[("xt", tt % 2), "ident_f"], writes=[PS(bank)] if i == 0 else [], sig=(i == 3))
                        p.lastw[PS(bank)] = ("s_pe", p.cnt["pe"], "pe")
                        p.op("dve", lambda f4=f4, tt=tt, bank=bank: nc.vector.tensor_copy(
                            xT[:, f4 * 4:f4 * 4 + 4, tt * 128:(tt + 1) * 128], ps[bank][:].rearrange("p (a b) -> p a b", a=4)),
                            reads=[PS(bank)], writes=[("xT", f4 * 4 + i) for i in range(4)])
                for k in range(10):
                    p.dma("sp", catc[:, k, :], catT[k, :, csl], reads=[("cat", k, c4)], writes=[("catc", k)] + [("yT", f) for f in range(16)])
                for mg in range(4):
                    w = wb[wl % 2]
                    wkey = ("wb", wl % 2)
                    wl += 1
                    wv = w[:].rearrange("p (j x) -> p j x", j=4)[:, :, 0:1280].rearrange("p j (k m) -> p j k m", k=10)
                    for j in range(4):
                        p.dma("sp", wv[:, j, :, :], w_out_b[mg * 4 + j].rearrange("p (k m) -> p k m", k=10), writes=[(wkey, j)])
                    for j in range(4):
                        mt = mg * 4 + j
                        bank = (2, 3, 6, 7)[mt % 4]
                        for kt in range(10):
                            p.op("pe", lambda wv=wv, j=j, kt=kt, bank=bank: nc.tensor.matmul(ps[bank][:], wv[:, j, kt, :], catc[:, kt, :], start=(kt == 0), stop=(kt == 9)),
                                 reads=[(wkey, j), ("catc", kt)], writes=[PS(bank)] if kt == 0 else [], sig=(kt == 9))
                        p.lastw[PS(bank)] = ("s_pe", p.cnt["pe"], "pe")
                        p.op("dve", lambda mt=mt, bank=bank: nc.vector.tensor_copy(mixT[:, mt, :], ps[bank][:]),
                             reads=[PS(bank)], writes=[("mixT", mt)] + ALL_A)
                        q2 = sq3[mt % 2]
                        p.op("act", lambda mt=mt, bank=bank, q2=q2: nc.scalar.activation(q2[:], ps[bank][:], AF.Square),
                             reads=[PS(bank)], writes=[("sq3", mt % 2)])
                        p.op("pe", lambda mt=mt, q2=q2: nc.tensor.matmul(ps[4][:], ones_b[:], q2[:], start=(mt == 0), stop=(mt == 15)),
                             reads=["ones_b", ("sq3", mt % 2)], writes=[PS(4)] if mt == 0 else [])
                p.lastw[PS(4)] = ("s_pe", p.cnt["pe"], "pe")
                rstd_rep(rrep, ps[4][:], float(D), PS(4), "rrep3", rtmp, "rtmp3")
                for f in range(16):
                    t_ = tm3[f % 2]
                    q2 = sq3[f % 2]
                    p.op("dve", lambda f=f, t_=t_: nc.vector.tensor_tensor(t_[:], mixT[:, f, :], rrep[:], ALU.mult),
                         reads=[("mixT", f), "rrep3"], writes=[("tm3", f % 2)])
                    p.op("dve", lambda f=f, t_=t_: nc.vector.scalar_tensor_tensor(xT[:, f, :], t_[:], modc[:, G1 + f:G1 + f + 1], xT[:, f, :], ALU.mult, ALU.add),
                         reads=[("tm3", f % 2), "modc", ("xT", f)], writes=[("xT", f)])
                    p.op("act", lambda f=f, q2=q2: nc.scalar.activation(q2[:], xT[:, f, :], AF.Square),
                         reads=[("xT", f)], writes=[("sq3", f % 2)])
                    p.op("pe", lambda f=f, q2=q2: nc.tensor.matmul(ps[5][:], ones_b[:], q2[:], start=(f == 0), stop=(f == 15)),
                         reads=["ones_b", ("sq3", f % 2)], writes=[PS(5)] if f == 0 else [])
                p.lastw[PS(5)] = ("s_pe", p.cnt["pe"], "pe")
                rstd_rep(rrep, ps[5][:], float(D), PS(5), "rrep3", rtmp, "rtmp3")
                for f in range(16):
                    t_ = tm3[f % 2]
                    p.op("dve", lambda f=f, t_=t_: nc.vector.scalar_tensor_tensor(t_[:], xT[:, f, :], modc[:, A2 + f:A2 + f + 1], rrep[:], ALU.mult, ALU.mult),
                         reads=[("xT", f), "modc", "rrep3"], writes=[("tm3", f % 2)])
                    p.op("act", lambda f=f, t_=t_: nc.scalar.activation(h2T[:, f, :], t_[:], AF.Identity, bias=modc[:, B2 + f:B2 + f + 1]),
                         reads=[("tm3", f % 2), "modc"], writes=[("h2T", f)] + [("yT", i) for i in range(16)])
                for jg in range(16):
                    w = wb[wl % 2]
                    wkey = ("wb", wl % 2)
                    wl += 1
                    wv = w[:].rearrange("p (j k m) -> p j k m", j=4, k=16)
                    for j in range(4):
                        p.dma("sp", wv[:, j, :, :], w_m1_b[jg * 4 + j].rearrange("p (k m) -> p k m", k=16), writes=[(wkey, j)])
                    for j in range(4):
                        jt = jg * 4 + j
                        bank = (2, 3, 6, 7)[jt % 4]
                        for kt in range(16):
                            p.op("pe", lambda wv=wv, j=j, kt=kt, bank=bank: nc.tensor.matmul(ps[bank][:], wv[:, j, kt, :], h2T[:, kt, :], start=(kt == 0), stop=(kt == 15)),
                                 reads=[(wkey, j), ("h2T", kt)], writes=[PS(bank)] if kt == 0 else [], sig=(kt == 15))
                        p.lastw[PS(bank)] = ("s_pe", p.cnt["pe"], "pe")
                        r_ = rl3[jt % 2]
                        p.op("act", lambda bank=bank, r_=r_: nc.scalar.activation(r_[:], ps[bank][:], AF.Relu),
                             reads=[PS(bank)], writes=[("rl3", jt % 2)])
                        p.op("dve", lambda jt=jt, r_=r_: nc.vector.tensor_tensor(aT[:, jt, :], r_[:], r_[:], ALU.mult),
                             reads=[("rl3", jt % 2)], writes=[("aT", jt)] + ([("mixT", f) for f in range(16)] if jt < 32 else []))
                for mt in range(16):
                    w = wb[wl % 2]
                    wkey = ("wb", wl % 2)
                    wl += 1
                    wv = w[:].rearrange("p (j m) -> p j m", j=64)
                    p.dma("sp", wv, w_m2_b[mt].rearrange("p (j m) -> p j m", j=64), writes=[(wkey, j_) for j_ in range(4)])
                    bank = (2, 3, 6, 7)[mt % 4]
                    for jt in range(64):
                        p.op("pe", lambda wv=wv, jt=jt, bank=bank: nc.tensor.matmul(ps[bank][:], wv[:, jt, :], aT[:, jt, :], start=(jt == 0), stop=(jt == 63)),
                             reads=[(wkey, 0), (wkey, 1), (wkey, 2), (wkey, 3), ("aT", jt)], writes=[PS(bank)] if jt == 0 else [], sig=(jt == 63))
                    p.lastw[PS(bank)] = ("s_pe", p.cnt["pe"], "pe")
                    p.op("dve", lambda mt=mt, bank=bank: nc.vector.tensor_copy(yT[:, mt, :], ps[bank][:]),
                         reads=[PS(bank)], writes=[("yT", mt)] + ALL_H2)
                    q2 = sq3[mt % 2]
                    p.op("act", lambda bank=bank, q2=q2: nc.scalar.activation(q2[:], ps[bank][:], AF.Square),
                         reads=[PS(bank)], writes=[("sq3", mt % 2)])
                    p.op("pe", lambda mt=mt, q2=q2: nc.tensor.matmul(ps[4][:], ones_b[:], q2[:], start=(mt == 0), stop=(mt == 15)),
                         reads=["ones_b", ("sq3", mt % 2)], writes=[PS(4)] if mt == 0 else [])
                p.lastw[PS(4)] = ("s_pe", p.cnt["pe"], "pe")
                rstd_rep(rrep, ps[4][:], float(D), PS(4), "rrep3", rtmp, "rtmp3")
                for f in range(16):
                    t_ = tm3[f % 2]
                    p.op("dve", lambda f=f, t_=t_: nc.vector.tensor_tensor(t_[:], yT[:, f, :], rrep[:], ALU.mult),
                         reads=[("yT", f), "rrep3"], writes=[("tm3", f % 2)])
                    p.op("dve", lambda f=f, t_=t_: nc.vector.scalar_tensor_tensor(xT[:, f, :], t_[:], modc[:, G2 + f:G2 + f + 1], xT[:, f, :], ALU.mult, ALU.add),
                         reads=[("tm3", f % 2), "modc", ("xT", f)], writes=[("xT", f)])
                for tt in range(4):
                    row0 = c4 * NT + tt * 128
                    ob = xt[tt % 2]
                    for f4 in range(4):
                        bank = f4 % 2
                        for i in range(4):
                            f = f4 * 4 + i
                            p.op("pe", lambda f=f, i=i, bank=bank, tt=tt: nc.tensor.transpose(
                                ps[bank][:, i * 128:(i + 1) * 128], xT[:, f, tt * 128:(tt + 1) * 128], ident_f[:]),
                                reads=[("xT", f), "ident_f"], writes=[PS(bank)] if i == 0 else [], sig=(i == 3))
                        p.lastw[PS(bank)] = ("s_pe", p.cnt["pe"], "pe")
                        p.op("act", lambda f4=f4, ob=ob, bank=bank: nc.scalar.copy(ob[:, f4 * 512:(f4 + 1) * 512], ps[bank][:]),
                             reads=[PS(bank)], writes=[("xt", tt % 2)])
                    p.dma("sp", out[row0:row0 + 128, :], ob[:], reads=[("xt", tt % 2)], writes=[("out", c4, tt)])


def _host_layouts(inp):
    f32 = np.float32
    L = {}
    w_ada = inp["w_ada"][0]
    L["w_ada_t"] = np.ascontiguousarray(w_ada.reshape(16, 128, 24, 512).transpose(2, 1, 0, 3)).reshape(24, 128, 16 * 512)
    L["b_ada_col"] = np.ascontiguousarray(inp["b_ada"][0].reshape(96, 128).T)
    col16 = lambda v: v.reshape(-1, 128).T
    L["gcols"] = np.ascontiguousarray(np.concatenate([
        col16(inp["g_pre_mix"][0]), col16(inp["g_post_mix"][0]), col16(inp["g_pre_mlp"][0]), col16(inp["g_post_mlp"][0]),
        col16(inp["g_attn_out"][0]), col16(inp["g_ssm_out"][0]), col16(inp["b_glu"][0]), col16(inp["ssm_d"][0].reshape(-1))], axis=1).astype(f32))
    L["w_in_t"] = np.ascontiguousarray(inp["w_in"][0].reshape(16, 128, 22, 128).transpose(2, 1, 0, 3)).reshape(22, 128, 2048)
    L["w_out_t"] = np.ascontiguousarray(inp["w_out"][0].reshape(10, 128, 16, 128).transpose(2, 1, 0, 3)).reshape(16, 128, 1280)
    L["w_m1_t"] = np.ascontiguousarray(inp["w_mlp_in"][0].reshape(16, 128, 64, 128).transpose(2, 1, 0, 3)).reshape(64, 128, 2048)
    L["w_m2_t"] = np.ascontiguousarray(inp["w_mlp_out"][0].reshape(64, 128, 16, 128).transpose(2, 1, 0, 3)).reshape(16, 128, 8192)
    L["w_glu_t"] = np.ascontiguousarray(inp["w_glu"][0].reshape(7, 128, 7, 128).transpose(2, 1, 0, 3)).reshape(7, 128, 896)
    a_re, a_im, ldt = inp["ssm_a_re"][0], inp["ssm_a_im"][0], inp["ssm_log_dt"][0]
    b_re, b_im = inp["ssm_b_re"][0], inp["ssm_b_im"][0]
    c_re, c_im = inp["ssm_c_re"][0], inp["ssm_c_im"][0]
    alay = np.zeros((128, 3, 7, 4, 64), f32)
    bexp = np.zeros((128, 2, 7, 4, 64), f32)
    cexp = np.zeros((128, 2, 56, 64), f32)
    for o in range(7):
        for hb in range(2):
            for gq in range(4):
                g = 8 * o + 4 * hb + gq
                rows = slice(64 * hb, 64 * hb + 64)
                alay[rows, 0, o, gq, :] = a_re[g][None, :]
                alay[rows, 1, o, gq, :] = a_im[g][None, :]
                alay[rows, 2, o, gq, :] = ldt[g]
                r0 = 64 * hb + 16 * gq
                bexp[r0:r0 + 16, 0, o, gq, :] = b_re[g].T
                bexp[r0:r0 + 16, 1, o, gq, :] = b_im[g].T
                for half in range(2):
                    lr = slice(64 * half, 64 * half + 64)
                    cexp[lr, 0, g, 16 * gq:16 * gq + 16] = c_re[g].T
                    cexp[lr, 1, g, 16 * gq:16 * gq + 16] = c_im[g].T
    L["alay"] = alay.reshape(128, -1)
    L["bexp"] = bexp.reshape(128, -1)
    L["cexp"] = cexp.reshape(128, -1)
    lanes = np.zeros((128, 3, 56), f32)
    lanes[:, 0, :] = np.concatenate([a_re.T, a_re.T], 0)
    lanes[:, 1, :] = np.concatenate([a_im.T, a_im.T], 0)
    lanes[:, 2, :] = ldt[None, :]
    L["lanes"] = lanes.reshape(128, -1)
    L["ident"] = np.eye(128, dtype=f32)
    e = np.arange(128) % 64
    rm = np.zeros((128, 128), f32)
    for m in range(128):
        if e[m] < 8:
            rm[m + 8, m] = 1.0
        elif e[m] < 16:
            rm[m - 8, m] = 1.0
    L["rmat"] = rm
    kk = np.arange(128)[:, None]
    qq = np.arange(128)[None, :]
    L["tri_cur"] = np.where(kk <= qq, 1.0, 0.0).astype(f32)
    L["tri_prev"] = np.where(kk >= qq, 1.0, 0.0).astype(f32)
    L["jidx"] = np.broadcast_to(np.arange(NT, dtype=f32)[None, :], (128, NT)).copy()
    freq = (500000.0 ** (-(2.0 * (e % 8)) / 16.0)).astype(np.float64)
    cc = np.zeros((128, 8), f32)
    cc[:, 0] = (freq / (2 * np.pi)).astype(f32)
    cc[:, 1] = (e < 16)
    cc[:, 2] = 1.0 - cc[:, 1]
    cc[:, 3] = np.where(e < 8, -1.0, np.where(e < 16, 1.0, 0.0))
    cc[:, 5] = PI / 2
    L["cc"] = cc
    return L


_CACHE = {}


def kernel(**inputs):
    inp = {k: np.asarray(v) for k, v in inputs.items()}
    if "nc" not in _CACHE:
        _CACHE["nc"] = build_program()[0]
    nc = _CACHE["nc"]
    L = _host_layouts(inp)
    x = inp["x"]
    pos = inp["positions"].astype(np.int32)
    in_maps = []
    shared = {k: L[k] for k in ("w_ada_t", "b_ada_col", "gcols", "w_in_t", "w_out_t", "w_m1_t", "w_m2_t", "w_glu_t",
                                "alay", "bexp", "cexp", "lanes", "ident", "rmat", "jidx")}
    for core in range(8):
        b, h = core // 2, core % 2
        m = dict(shared)
        m["x_own"] = np.ascontiguousarray(x[b, h * 2048:(h + 1) * 2048])
        m["x_prev"] = np.ascontiguousarray(x[b, 0:2048]) if h == 1 else np.zeros((2048, D), np.float32)
        pw = np.zeros((1, 4096), np.int32)
        pw[0, 2048:] = pos[b, h * 2048:(h + 1) * 2048]
        if h == 1:
            pw[0, :2048] = pos[b, 0:2048]
        m["pos"] = pw
        m["c_col"] = np.ascontiguousarray(inp["c"][b].reshape(16, 128).T)
        cc = L["cc"].copy()
        cc[:, 4] = float(h)
        m["cc"] = cc
        prev0 = L["tri_prev"] if h == 1 else np.zeros((128, 128), np.float32)
        m["masks"] = np.ascontiguousarray(np.concatenate([L["tri_prev"], L["tri_cur"], prev0, L["tri_cur"]], axis=1))
        in_maps.append(m)
    res = run_bass_kernel_spmd(nc, in_maps, core_ids=list(range(8)))
    outp = np.zeros((4, 4096, D), np.float32)
    for core in range(8):
        b, h = core // 2, core % 2
        outp[b, h * 2048:(h + 1) * 2048] = res.results[core]["out"]
    return outp
```

```python
import contextlib
import numpy as np
import concourse.bass as bass
import concourse.mybir as mybir
from concourse.bass_utils import run_bass_kernel_spmd

F32 = mybir.dt.float32
BF16 = mybir.dt.bfloat16
I32 = mybir.dt.int32
AF = mybir.ActivationFunctionType
ALU = mybir.AluOpType

D = 2048
NT = 512
NCH = 8
PI = float(np.pi)
MAGIC = 12582912.0
NEG = -30000.0
EPS = 1e-6


class Prog:
    NDMA = 8

    def __init__(self, nc, es):
        self.nc = nc
        self.eh = dict(pe=nc.tensor, act=nc.scalar, dve=nc.vector, pool=nc.gpsimd, sp=nc.sync)
        self.sem = {e: es.enter_context(nc.semaphore("s_" + e)) for e in self.eh}
        self.cnt = {e: 0 for e in self.eh}
        self.seen = {e: {} for e in self.eh}
        self.dsem = {}
        self.dcnt = {}
        self.drr = {}
        self.semobj = {}
        for e in self.eh:
            self.semobj["s_" + e] = self.sem[e]
        for q in ("sp", "pool"):
            self.dsem[q] = [es.enter_context(nc.semaphore("d_%s%d" % (q, i))) for i in range(self.NDMA)]
            self.dcnt[q] = [0] * self.NDMA
            self.drr[q] = 0
            for i, s in enumerate(self.dsem[q]):
                self.semobj["d_%s%d" % (q, i)] = s
        self.lastw = {}
        self.readers = {}
        self.nwait = 0
        self.nins = 0
        self.dead = False

    def _need(self, eng, reads, writes):
        need = {}

        def add(t, same_ok):
            if t is None:
                return
            sn, val, e = t
            if same_ok and e == eng and eng == "pe":
                return
            if need.get(sn, 0) < val:
                need[sn] = val
        for k in reads:
            add(self.lastw.get(k), False)
        for k in writes:
            add(self.lastw.get(k), True)
            for sn, (val, e) in self.readers.get(k, {}).items():
                add((sn, val, e), True)
        return need

    def _emit_waits(self, eng, need):
        h = self.eh[eng]
        seen = self.seen[eng]
        for sn, val in need.items():
            if seen.get(sn, 0) >= val:
                continue
            h.wait_ge(self.semobj[sn], val)
            seen[sn] = val
            self.nwait += 1

    def _record(self, ticket, reads, writes):
        sn, val, e = ticket
        for k in reads:
            d = self.readers.setdefault(k, {})
            if d.get(sn, (0, None))[0] < val:
                d[sn] = (val, e)
        for k in writes:
            self.lastw[k] = ticket
            self.readers[k] = {}

    def op(self, eng, fn, reads=(), writes=(), sig=True):
        if self.dead:
            return None
        psr = [k for k in reads if isinstance(k, tuple) and k[0] == "ps"]
        if psr:
            writes = list(writes) + psr
        need = self._need(eng, reads, writes)
        self._emit_waits(eng, need)
        ins = fn()
        self.nins += 1
        sn = "s_" + eng
        if sig:
            self.cnt[eng] += 1
            ins.then_inc(self.sem[eng], 1)
            ticket = (sn, self.cnt[eng], eng)
        else:
            ticket = (sn, self.cnt[eng] + 1, eng)
        self._record(ticket, reads, writes)
        return ins

    def dma(self, q, out, in_, reads=(), writes=()):
        if self.dead:
            return None
        need = self._need(q, reads, writes)
        i = self.drr[q]
        self.drr[q] = (i + 1) % self.NDMA
        sn = "d_%s%d" % (q, i)
        if self.dcnt[q][i] > 0 and need.get(sn, 0) < self.dcnt[q][i]:
            need[sn] = self.dcnt[q][i]
        self._emit_waits(q, need)
        ins = self.eh[q].dma_start(out=out, in_=in_)
        self.dcnt[q][i] += 16
        ins.then_inc(self.dsem[q][i], 16)
        self.nins += 1
        self._record((sn, self.dcnt[q][i], None), reads, writes)
        return ins

    def barrier(self):
        if self.dead:
            return
        need = {}
        for e in self.eh:
            if self.cnt[e] > 0:
                need["s_" + e] = self.cnt[e]
        for q in self.dsem:
            for i in range(self.NDMA):
                if self.dcnt[q][i] > 0:
                    need["d_%s%d" % (q, i)] = self.dcnt[q][i]
        for e in self.eh:
            n2 = {k: v for k, v in need.items() if k != "s_" + e}
            self._emit_waits(e, n2)
        self.lastw = {}
        self.readers = {}


def build_program(debug=(), stop_after=None):
    nc = bass.Bass("TRN2", target_bir_lowering=False)
    dbg = {}

    def din(name, shape, dt=F32):
        return nc.dram_tensor(name, list(shape), dt, kind="ExternalInput").ap()

    x_own = din("x_own", [2048, D])
    x_prev = din("x_prev", [2048, D])
    pos_in = din("pos", [1, 4096], I32)
    c_col_in = din("c_col", [128, 16])
    cc_in = din("cc", [128, 8])
    masks_in = din("masks", [128, 4 * 128])
    ident_in = din("ident", [128, 128])
    rmat_in = din("rmat", [128, 128])
    jidx_in = din("jidx", [128, NT])
    w_ada_in = din("w_ada_t", [24, 128, 16 * 512])
    b_ada_in = din("b_ada_col", [128, 96])
    gcols_in = din("gcols", [128, 4 * 16 + 3 + 7 + 7 + 7])
    w_in_in = din("w_in_t", [22, 128, 16 * 128])
    w_out_in = din("w_out_t", [16, 128, 10 * 128])
    w_m1_in = din("w_m1_t", [64, 128, 16 * 128])
    w_m2_in = din("w_m2_t", [16, 128, 64 * 128])
    w_glu_in = din("w_glu_t", [7, 128, 7 * 128])
    alay_in = din("alay", [128, 3 * 1792])
    bexp_in = din("bexp", [128, 2 * 1792])
    cexp_in = din("cexp", [128, 2 * 3584])
    lanes_in = din("lanes", [128, 3 * 56])
    out = nc.dram_tensor("out", [2048, D], F32, kind="ExternalOutput").ap()
    catT = nc.dram_tensor("catT_scr", [10, 128, 2048], BF16, kind="Internal").ap()
    uT_scr = nc.dram_tensor("uT_scr", [7, 128, 4096], BF16, kind="Internal").ap()
    w_m1_b = nc.dram_tensor("w_m1_b", [64, 128, 16 * 128], BF16, kind="Internal").ap()
    w_m2_b = nc.dram_tensor("w_m2_b", [16, 128, 64 * 128], BF16, kind="Internal").ap()
    w_out_b = nc.dram_tensor("w_out_b", [16, 128, 10 * 128], BF16, kind="Internal").ap()

    def dout(name, shape, dt=F32):
        dbg[name] = nc.dram_tensor("dbg_" + name, list(shape), dt, kind="ExternalOutput").ap()
        return dbg[name]

    def ckpt(name):
        if stop_after == name:
            p.dead = True

    with contextlib.ExitStack() as es:
        p = Prog(nc, es)
        _phases(nc, es, p, ckpt, debug, dout, locals())
        p.dead = False
        p.barrier()
    build_program.stats = (p.nins, p.nwait)
    return nc, dbg


def _phases(nc, es, p, ckpt, debug, dout, env):
    globals_needed = ("x_own", "x_prev", "pos_in", "c_col_in", "cc_in", "masks_in", "ident_in", "rmat_in", "jidx_in",
                      "w_ada_in", "b_ada_in", "gcols_in", "w_in_in", "w_out_in", "w_m1_in", "w_m2_in", "w_glu_in",
                      "alay_in", "bexp_in", "cexp_in", "lanes_in", "out", "catT", "w_m1_b", "w_m2_b", "w_out_b", "uT_scr")
    (x_own, x_prev, pos_in, c_col_in, cc_in, masks_in, ident_in, rmat_in, jidx_in, w_ada_in, b_ada_in, gcols_in,
     w_in_in, w_out_in, w_m1_in, w_m2_in, w_glu_in, alay_in, bexp_in, cexp_in, lanes_in, out, catT, w_m1_b, w_m2_b, w_out_b, uT_scr) = (env[k] for k in globals_needed)
    if True:
        _tn = [0]

        def T(st, name, shape, dt):
            _tn[0] += 1
            return st.enter_context(nc.sbuf_tensor("t%d_%s" % (_tn[0], name), list(shape), dt))
        ps = [es.enter_context(nc.psum_tensor("psb%d" % i, [128, 512], F32)) for i in range(8)]
        psb = [t[:].bitcast(BF16) for t in ps]
        PS = lambda i: ("ps", i)

        ident_f = T(es, "ident_f", [128, 128], F32)
        ident_b = T(es, "ident_b", [128, 128], BF16)
        ones_b = T(es, "ones_b", [128, 128], BF16)
        rmat_b = T(es, "rmat_b", [128, 128], BF16)
        masks_b = T(es, "masks_b", [128, 512], BF16)
        cc = T(es, "cc", [128, 8], F32)
        jidx = T(es, "jidx", [128, NT], F32)
        gcols = T(es, "gcols", [128, 88], F32)
        modc = T(es, "modc", [128, 96], F32)
        p.dma("sp", ident_f[:], ident_in[:, :], writes=["ident_f"])
        p.dma("pool", ident_b[:], ident_in[:, :], writes=["ident_b"])
        p.dma("pool", rmat_b[:], rmat_in[:, :], writes=["rmat_b"])
        p.dma("pool", masks_b[:], masks_in[:, :], writes=["masks_b"])
        p.dma("sp", cc[:], cc_in[:, :], writes=["cc"])
        p.dma("sp", jidx[:], jidx_in[:, :], writes=["jidx"])
        p.dma("sp", gcols[:], gcols_in[:, :], writes=["gcols"])
        p.op("dve", lambda: nc.vector.memset(ones_b[:], 1.0), writes=["ones_b"])
        conv_list = [(w_out_b[i], w_out_in[i], ("cv_out", i)) for i in range(16)] + \
                    [(w_m1_b[i], w_m1_in[i], ("cv_m1", i)) for i in range(64)] + \
                    [(w_m2_b[i], w_m2_in[i], ("cv_m2", i)) for i in range(16)]
        conv_pos = [0]

        def conv_step(n=1):
            for _ in range(n):
                if conv_pos[0] < len(conv_list):
                    dst, src, key = conv_list[conv_pos[0]]
                    conv_pos[0] += 1
                    p.dma("pool", dst, src, writes=[key])
        FREQT, RM, ORM, SG, FLAG = 0, 1, 2, 3, 4
        G_PRE_MIX, G_POST_MIX, G_PRE_MLP, G_POST_MLP, G_ATT, G_SSM, B_GLU, DSK = 0, 16, 32, 48, 64, 67, 74, 81
        A1, B1, G1, A2, B2, G2 = 0, 16, 32, 48, 64, 80

        with contextlib.ExitStack() as ph:
            c_col = T(ph, "c_col", [128, 16], F32)
            s_col = T(ph, "s_col", [128, 16], BF16)
            b_ada = T(ph, "b_ada", [128, 96], F32)
            modr = T(ph, "modr", [128, 96], F32)
            wa = [T(ph, "wa%d" % i, [128, 16, 512], BF16) for i in range(2)]
            p.dma("sp", c_col[:], c_col_in[:, :], writes=["c_col"])
            p.dma("sp", b_ada[:], b_ada_in[:, :], writes=["b_ada"])
            p.op("act", lambda: nc.scalar.activation(s_col[:], c_col[:], AF.Silu), reads=["c_col"], writes=["s_col"])
            for mg in range(24):
                w = wa[mg % 2]
                p.dma("pool", w[:], w_ada_in[mg].rearrange("p (k m) -> p k m", k=16), writes=[("wa", mg % 2)])
                for m4 in range(4):
                    col = mg * 4 + m4
                    for kt in range(16):
                        p.op("pe", lambda w=w, m4=m4, kt=kt, col=col: nc.tensor.matmul(
                            ps[0][:, col:col + 1], w[:, kt, m4 * 128:(m4 + 1) * 128], s_col[:, kt:kt + 1],
                            start=(kt == 0), stop=(kt == 15)),
                            reads=[("wa", mg % 2), "s_col"], writes=[PS(0)] if (mg == 0 and m4 == 0 and kt == 0) else [],
                            sig=(kt == 15))
            p.lastw[PS(0)] = ("s_pe", p.cnt["pe"], "pe")
            p.op("dve", lambda: nc.vector.tensor_tensor(modr[:], ps[0][:, 0:96], b_ada[:], ALU.add),
                 reads=[PS(0), "b_ada"], writes=["modr"])
            for (dst, gsrc, col) in ((A1, G_PRE_MIX, 16), (A2, G_PRE_MLP, 64)):
                p.op("dve", lambda dst=dst, gsrc=gsrc, col=col: nc.vector.scalar_tensor_tensor(
                    modc[:, dst:dst + 16], modr[:, col:col + 16], 1.0, gcols[:, gsrc:gsrc + 16], ALU.add, ALU.mult),
                    reads=["modr", "gcols"], writes=["modc"])
            for (dst, col) in ((B1, 0), (B2, 48)):
                p.op("dve", lambda dst=dst, col=col: nc.vector.tensor_copy(modc[:, dst:dst + 16], modr[:, col:col + 16]),
                     reads=["modr"], writes=["modc"])
            for (dst, gsrc, col) in ((G1, G_POST_MIX, 32), (G2, G_POST_MLP, 80)):
                p.op("dve", lambda dst=dst, gsrc=gsrc, col=col: nc.vector.tensor_tensor(
                    modc[:, dst:dst + 16], modr[:, col:col + 16], gcols[:, gsrc:gsrc + 16], ALU.mult),
                    reads=["modr", "gcols"], writes=["modc"])
            if "mod" in debug:
                p.dma("sp", dout("mod", [128, 96])[:, :], modr[:], reads=["modr"], writes=["o_mod"])
                p.dma("sp", dout("modc", [128, 96])[:, :], modc[:], reads=["modc"], writes=["o_modc"])
            p.barrier()
            ckpt("p0")

        def hT_steps(st_tiles, ci, acol, bcol, hkey="hT"):
            xt, sqj, xn, rs, hT = st_tiles
            src = x_prev if ci < 4 else x_own
            steps = []

            def s1a(tt):
                row0 = (ci % 4) * NT + tt * 128
                p.dma("sp", xt[tt % 2][:], src[row0:row0 + 128, :], writes=[("xt", tt % 2)])

            def s1(tt):
                xb_ = xt[tt % 2]
                p.op("act", lambda: nc.scalar.activation(sqj[:], xb_[:], AF.Square, accum_out=rs[:, 4 * (tt % 2):4 * (tt % 2) + 1]),
                     reads=[("xt", tt % 2)], writes=["sqj", ("rs0", tt % 2)])
                p.op("act", lambda: nc.scalar.activation(rs[:, 4 * (tt % 2) + 1:4 * (tt % 2) + 2], rs[:, 4 * (tt % 2):4 * (tt % 2) + 1], AF.Ln, bias=EPS, scale=1.0 / D),
                     reads=[("rs0", tt % 2)], writes=[("rs1", tt % 2)])
                p.op("act", lambda: nc.scalar.activation(rs[:, 4 * (tt % 2) + 2:4 * (tt % 2) + 3], rs[:, 4 * (tt % 2) + 1:4 * (tt % 2) + 2], AF.Exp, scale=-0.5),
                     reads=[("rs1", tt % 2)], writes=[("rs2", tt % 2)])

            def s2(tt):
                xb_ = xt[tt % 2]
                p.op("dve", lambda: nc.vector.tensor_scalar(xn[:], xb_[:], rs[:, 4 * (tt % 2) + 2:4 * (tt % 2) + 3], None, ALU.mult),
                     reads=[("xt", tt % 2), ("rs2", tt % 2)], writes=["xn"])

            def s3(tt, half, part):
                if True:
                    bank = half
                    if part == 0:
                        for f8 in range(8):
                            f = half * 8 + f8
                            p.op("pe", lambda f=f, f8=f8: nc.tensor.transpose(
                                psb[bank][:, f8 * 128:(f8 + 1) * 128], xn[:, f * 128:(f + 1) * 128], ident_b[:]),
                                reads=["xn", "ident_b"], writes=[PS(bank)] if f8 == 0 else [], sig=(f8 == 7))
                        p.lastw[PS(bank)] = ("s_pe", p.cnt["pe"], "pe")
                    for f8 in range(4 * part, 4 * part + 4):
                        f = half * 8 + f8
                        p.op("act", lambda f=f, f8=f8: nc.scalar.activation(
                            hT[:, f, tt * 128:(tt + 1) * 128], psb[bank][:, f8 * 128:(f8 + 1) * 128], AF.Identity,
                            bias=modc[:, bcol + f:bcol + f + 1], scale=modc[:, acol + f:acol + f + 1]),
                            reads=[PS(bank), "modc"], writes=[(hkey, f)])
            steps.append(lambda: s1a(0))
            steps.append(lambda: s1a(1))
            for tt in range(4):
                steps.append(lambda tt=tt: s1(tt))
                steps.append(lambda tt=tt: s2(tt))
                steps.append(lambda tt=tt: s3(tt, 0, 0))
                steps.append(lambda tt=tt: s3(tt, 0, 1))
                steps.append(lambda tt=tt: s3(tt, 1, 0))
                steps.append(lambda tt=tt: s3(tt, 1, 1))
                if tt + 2 < 4:
                    steps.append(lambda tt=tt: s1a(tt + 2))
            return steps

        def make_hT(st_tiles, ci, acol, bcol):
            for st_ in hT_steps(st_tiles, ci, acol, bcol):
                st_()

        def hT_tiles(st):
            xt = [T(st, "xt%d" % i, [128, D], F32) for i in range(2)]
            sqj = T(st, "sqj", [128, D], BF16)
            xn = T(st, "xn", [128, D], BF16)
            rs = T(st, "rs", [128, 8], F32)
            hT = T(st, "hT", [128, 16, NT], BF16)
            return (xt, sqj, xn, rs, hT)

        def rstd_rep(dst, src_ps, n, key_src, key_dst, tmp, key_tmp):
            p.op("act", lambda: nc.scalar.activation(tmp[:], src_ps, AF.Ln, bias=EPS, scale=1.0 / n),
                 reads=[key_src], writes=[key_tmp])
            p.op("act", lambda: nc.scalar.activation(dst[:], tmp[:], AF.Exp, scale=-0.5),
                 reads=[key_tmp], writes=[key_dst])

        with contextlib.ExitStack() as pa:
            qT = T(pa, "qT", [128, 9, 2048], BF16)
            kT = T(pa, "kT", [128, 3, 4096], BF16)
            vT = T(pa, "vT", [128, 3, 4096], BF16)
            with contextlib.ExitStack() as ph:
                tiles0 = hT_tiles(ph)
                hT_b = [tiles0[4], T(ph, "hT1", [128, 16, NT], BF16)]
                wq = [T(ph, "wq%d" % i, [128, 3, 16, 128], BF16) for i in range(2)]
                posi = T(ph, "posi", [128, NT], I32)
                ra = T(ph, "ra", [128, NT], F32)
                rb = T(ph, "rb", [128, NT], F32)
                rS = T(ph, "rS", [128, NT], F32)
                rC = T(ph, "rC", [128, NT], F32)
                cosk_b = [T(ph, "cosk%d" % i, [128, NT], F32) for i in range(2)]
                sink_b = [T(ph, "sink%d" % i, [128, NT], F32) for i in range(2)]
                cosq_b = [T(ph, "cosq%d" % i, [128, NT], F32) for i in range(2)]
                sinq_b = [T(ph, "sinq%d" % i, [128, NT], F32) for i in range(2)]
                xbf = [T(ph, "xbf%d" % i, [128, NT], BF16) for i in range(2)]
                t1 = [T(ph, "t1_%d" % i, [128, NT], F32) for i in range(2)]
                t2 = [T(ph, "t2_%d" % i, [128, NT], F32) for i in range(2)]

                def rope_steps(ci_):
                    pr = ci_ % 2
                    cosk, sink, cosq, sinq = cosk_b[pr], sink_b[pr], cosq_b[pr], sinq_b[pr]
                    ck, sk, cq, sq_ = ("cosk", pr), ("sink", pr), ("cosq", pr), ("sinq", pr)

                    def r1():
                        p.dma("sp", posi[:], pos_in[0:1, ci_ * NT:(ci_ + 1) * NT].to_broadcast([128, NT]), writes=["posi"])

                    def r2():
                        p.op("act", lambda: nc.scalar.activation(ra[:], posi[:], AF.Identity, scale=cc[:, FREQT:FREQT + 1]),
                             reads=["posi", "cc"], writes=["ra"])
                        p.op("dve", lambda: nc.vector.tensor_scalar(rb[:], ra[:], MAGIC, None, ALU.add), reads=["ra"], writes=["rb"])
                        p.op("dve", lambda: nc.vector.scalar_tensor_tensor(rb[:], rb[:], MAGIC, ra[:], ALU.subtract, ALU.subtract),
                             reads=["rb", "ra"], writes=["rb"])

                    def r3():
                        p.op("act", lambda: nc.scalar.activation(rS[:], rb[:], AF.Sin, scale=-2.0 * PI), reads=["rb"], writes=["rS"])
                        p.op("act", lambda: nc.scalar.activation(ra[:], rb[:], AF.Abs), reads=["rb"], writes=["ra"])
                        p.op("act", lambda: nc.scalar.activation(rC[:], ra[:], AF.Sin, scale=-2.0 * PI, bias=cc[:, 5:6]),
                             reads=["ra", "cc"], writes=["rC"])

                    def r4():
                        p.op("dve", lambda: nc.vector.tensor_scalar(cosk[:], rC[:], cc[:, RM:RM + 1], cc[:, ORM:ORM + 1], ALU.mult, ALU.add),
                             reads=["rC", "cc"], writes=[ck])
                        p.op("dve", lambda: nc.vector.tensor_scalar(sink[:], rS[:], cc[:, SG:SG + 1], None, ALU.mult),
                             reads=["rS", "cc"], writes=[sk])
                        if ci_ >= 4:
                            p.op("dve", lambda: nc.vector.tensor_scalar(cosq[:], cosk[:], 0.125, None, ALU.mult), reads=[ck], writes=[cq])
                            p.op("dve", lambda: nc.vector.tensor_scalar(sinq[:], sink[:], 0.125, None, ALU.mult), reads=[sk], writes=[sq_])
                    return [r1, r2, r3, r4]

                def prep1a(ci_):
                    tl = (tiles0[0], tiles0[1], tiles0[2], tiles0[3], hT_b[ci_ % 2])
                    return hT_steps(tl, ci_, A1, B1, hkey=("hT", ci_ % 2)) + rope_steps(ci_)

                for st_ in prep1a(0):
                    st_()
                wld = 0
                it = 0
                for ci in range(NCH):
                    own = ci >= 4
                    pr = ci % 2
                    hT = hT_b[pr]
                    hk = ("hT", pr)
                    cosk, sink, cosq, sinq = cosk_b[pr], sink_b[pr], cosq_b[pr], sinq_b[pr]
                    nxt = prep1a(ci + 1) if ci + 1 < NCH else []
                    groups = [0, 1, 2, 3, 4, 5, 6, 7] if own else [3, 4, 5, 6, 7]
                    npop = -(-len(nxt) // (3 * len(groups) - 2))
                    for g3 in groups:
                        w = wq[wld % 2]
                        wkey = ("wq", wld % 2)
                        wld += 1
                        nj = min(3, 22 - g3 * 3)
                        for j in range(nj):
                            p.dma("pool", w[:, j, :, :], w_in_in[g3 * 3 + j].rearrange("p (k m) -> p k m", k=16), writes=[(wkey, j)])
                        for j in range(nj):
                            for _ in range(npop):
                                if nxt:
                                    nxt.pop(0)()
                            mt = g3 * 3 + j
                            bank = 2 + (it % 2)
                            it += 1
                            for kt in range(16):
                                p.op("pe", lambda w=w, j=j, kt=kt, bank=bank: nc.tensor.matmul(
                                    ps[bank][:], w[:, j, kt, :], hT[:, kt, :], start=(kt == 0), stop=(kt == 15)),
                                    reads=[(wkey, j), (hk, kt)], writes=[PS(bank)] if kt == 0 else [], sig=(kt == 15))
                            p.lastw[PS(bank)] = ("s_pe", p.cnt["pe"], "pe")
                            csl = slice(ci * NT, (ci + 1) * NT)
                            if g3 == 4:
                                p.op("act", lambda bank=bank, j=j, csl=csl: nc.scalar.copy(vT[:, j, csl], ps[bank][:]),
                                     reads=[PS(bank)], writes=[("vT", j, ci)])
                                continue
                            if g3 >= 5:
                                o_ = mt - 15
                                xb_ = xbf[it % 2]
                                kx = ("xbf", it % 2)
                                if own:
                                    p.op("act", lambda bank=bank, xb_=xb_: nc.scalar.copy(xb_[:], ps[bank][:]), reads=[PS(bank)], writes=[kx])
                                else:
                                    p.op("act", lambda bank=bank, xb_=xb_: nc.scalar.activation(xb_[:], ps[bank][:], AF.Identity, scale=cc[:, FLAG:FLAG + 1]),
                                         reads=[PS(bank), "cc"], writes=[kx])
                                p.dma("sp", uT_scr[o_, :, csl], xb_[:], reads=[kx], writes=[("uscr", o_, ci)])
                                continue
                            isq = g3 < 3
                            ctab, stab = (cosq, sinq) if isq else (cosk, sink)
                            ck, sk = (("cosq", pr), ("sinq", pr)) if isq else (("cosk", pr), ("sink", pr))
                            b2 = 4 + (it % 2)
                            xb_ = xbf[it % 2]
                            a1 = t1[it % 2]
                            a2 = t2[it % 2]
                            kx, k1, k2 = ("xbf", it % 2), ("t1", it % 2), ("t2", it % 2)
                            p.op("act", lambda bank=bank, xb_=xb_: nc.scalar.copy(xb_[:], ps[bank][:]),
                                 reads=[PS(bank)], writes=[kx])
                            p.op("pe", lambda b2=b2, xb_=xb_: nc.tensor.matmul(ps[b2][:], rmat_b[:], xb_[:], start=True, stop=True),
                                 reads=["rmat_b", kx], writes=[PS(b2)])
                            p.op("dve", lambda bank=bank, a1=a1, ctab=ctab: nc.vector.tensor_tensor(a1[:], ps[bank][:], ctab[:], ALU.mult),
                                 reads=[PS(bank), ck], writes=[k1])
                            p.op("dve", lambda b2=b2, a2=a2, stab=stab: nc.vector.tensor_tensor(a2[:], ps[b2][:], stab[:], ALU.mult),
                                 reads=[PS(b2), sk], writes=[k2])
                            if isq:
                                dst = qT[:, mt, (ci - 4) * NT:(ci - 3) * NT]
                                dk = ("qT", mt, ci)
                            else:
                                dst = kT[:, j, csl]
                                dk = ("kT", j, ci)
                            p.op("pool", lambda dst=dst, a1=a1, a2=a2: nc.gpsimd.tensor_tensor(dst, a1[:], a2[:], ALU.add),
                                 reads=[k1, k2], writes=[dk])
                    while nxt:
                        nxt.pop(0)()
                if "qkv" in debug:
                    for j in range(9):
                        p.dma("sp", dout("qT%d" % j, [128, 2048], BF16)[:, :], qT[:, j, :], reads=[("qT", j, c) for c in range(4, 8)], writes=["o_q%d" % j])
                    for j in range(3):
                        p.dma("sp", dout("kT%d" % j, [128, 4096], BF16)[:, :], kT[:, j, :], reads=[("kT", j, c) for c in range(8)], writes=["o_k%d" % j])
                        p.dma("sp", dout("vT%d" % j, [128, 4096], BF16)[:, :], vT[:, j, :], reads=[("vT", j, c) for c in range(8)], writes=["o_v%d" % j])
                p.barrier()
                ckpt("p1a")

            with contextlib.ExitStack() as ph:
                vblk = T(ph, "vblk", [128, 32, 384], BF16)
                numacc = T(ph, "numacc", [128, 3, 2048], F32)
                denacc = T(ph, "denacc", [128, 3, 2048], F32)
                PT = [T(ph, "PT%d" % i, [128, 2, 256], BF16) for i in range(2)]
                attn = T(ph, "attn", [128, 3, NT], F32)
                sqa = T(ph, "sqa", [128, NT], BF16)
                rtmp = T(ph, "rtmp", [128, NT], F32)
                rrep = T(ph, "rrep", [128, NT], F32)
                atb = T(ph, "atb", [128, 3, NT], BF16)
                it = 0
                first = True
                for gi, d in ((2, 16), (1, 4), (0, 1)):
                    NB = 2048 // (128 * d)
                    nblk = d * (NB + 1)
                    for r in range(d):
                        for bb in range(NB + 1):
                            bi = r * (NB + 1) + bb
                            base = 2048 - 128 * d + r + 128 * d * bb
                            tb = 6 + (bi % 2)
                            for pt in range(3):
                                p.op("pe", lambda pt=pt, tb=tb, base=base, d=d: nc.tensor.transpose(
                                    psb[tb][:, pt * 128:(pt + 1) * 128], vT[:, pt, base:base + 127 * d + 1:d], ident_b[:]),
                                    reads=["ident_b"], writes=[PS(tb)] if pt == 0 else [], sig=(pt == 2))
                            p.lastw[PS(tb)] = ("s_pe", p.cnt["pe"], "pe")
                            p.op("dve", lambda bi=bi, tb=tb: nc.vector.tensor_copy(vblk[:, bi, :], psb[tb][:, 0:384]),
                                 reads=[PS(tb)], writes=[("vblk", bi)])
                    for r in range(d):
                        for b in range(NB):
                            qsl = slice(r + d * 128 * b, r + d * 128 * b + 127 * d + 1, d)
                            kcur = 2048 + r + d * 128 * b
                            kprev = kcur - 128 * d
                            bi_prev = r * (NB + 1) + b
                            bi_cur = bi_prev + 1
                            mprev = 2 if b == 0 else 1
                            for pt in range(3):
                                pbuf = PT[it % 2]
                                pk = ("PT", it % 2)
                                sb = [(it % 2) * 2, (it % 2) * 2 + 1]
                                ndb = 4 + (it % 2)
                                it += 1
                                for hp in range(2):
                                    rows = slice(64 * hp, 64 * hp + 64)
                                    for which, (kbase, mi) in enumerate(((kprev, mprev), (kcur, 0))):
                                        osl = slice(which * 128, which * 128 + 128)
                                        p.op("pe", lambda hp=hp, rows=rows, kbase=kbase, osl=osl, pt=pt, qsl=qsl, gi=gi, sb=sb, d=d: nc.tensor.matmul(
                                            ps[sb[hp]][:, osl], kT[rows, pt, kbase:kbase + 127 * d + 1:d], qT[rows, gi * 3 + pt, qsl],
                                            start=True, stop=True),
                                            reads=[], writes=[PS(sb[hp])] if which == 0 else [], sig=(which == 1))
                                    p.lastw[PS(sb[hp])] = ("s_pe", p.cnt["pe"], "pe")
                                    p.op("act", lambda hp=hp, pbuf=pbuf, sb=sb: nc.scalar.activation(pbuf[:, hp, :], ps[sb[hp]][:, 0:256], AF.Exp),
                                         reads=[PS(sb[hp])], writes=[(pk, hp)])
                                    moff = 256 if b == 0 else 0
                                    p.op("dve", lambda hp=hp, pbuf=pbuf, moff=moff: nc.vector.tensor_tensor(
                                        pbuf[:, hp, :], pbuf[:, hp, :], masks_b[:, moff:moff + 256], ALU.mult),
                                        reads=[(pk, hp), "masks_b"], writes=[(pk, hp)])
                                for hp in range(2):
                                    rows = slice(64 * hp, 64 * hp + 64)
                                    hcol = (2 * pt + hp) * 64
                                    for which, bi in enumerate((bi_prev, bi_cur)):
                                        p.op("pe", lambda rows=rows, hcol=hcol, bi=bi, which=which, pbuf=pbuf, hp=hp, ndb=ndb: nc.tensor.matmul(
                                            ps[ndb][rows, 0:128], vblk[:, bi, hcol:hcol + 64], pbuf[:, hp, which * 128:(which + 1) * 128],
                                            start=(which == 0), stop=(which == 1)),
                                            reads=[("vblk", bi), (pk, hp)], writes=[PS(ndb)] if (hp == 0 and which == 0) else [], sig=False)
                                    for which in range(2):
                                        p.op("pe", lambda rows=rows, which=which, pbuf=pbuf, hp=hp, ndb=ndb: nc.tensor.matmul(
                                            ps[ndb][rows, 128:256], ones_b[:, 0:64], pbuf[:, hp, which * 128:(which + 1) * 128],
                                            start=(which == 0), stop=(which == 1)),
                                            reads=["ones_b", (pk, hp)], writes=[], sig=(hp == 1 and which == 1))
                                p.lastw[PS(ndb)] = ("s_pe", p.cnt["pe"], "pe")
                                if first:
                                    p.op("dve", lambda pt=pt, qsl=qsl, ndb=ndb: nc.vector.tensor_copy(numacc[:, pt, qsl], ps[ndb][:, 0:128]),
                                         reads=[PS(ndb)], writes=[("num", pt)])
                                    p.op("dve", lambda pt=pt, qsl=qsl, ndb=ndb: nc.vector.tensor_copy(denacc[:, pt, qsl], ps[ndb][:, 128:256]),
                                         reads=[PS(ndb)], writes=[("den", pt)])
                                else:
                                    p.op("dve", lambda pt=pt, qsl=qsl, ndb=ndb: nc.vector.tensor_tensor(numacc[:, pt, qsl], ps[ndb][:, 0:128], numacc[:, pt, qsl], ALU.add),
                                         reads=[PS(ndb), ("num", pt)], writes=[("num", pt)])
                                    p.op("dve", lambda pt=pt, qsl=qsl, ndb=ndb: nc.vector.tensor_tensor(denacc[:, pt, qsl], ps[ndb][:, 128:256], denacc[:, pt, qsl], ALU.add),
                                         reads=[PS(ndb), ("den", pt)], writes=[("den", pt)])
                    first = False
                for c4 in range(4):
                    csl = slice(c4 * NT, (c4 + 1) * NT)
                    for pt in range(3):
                        p.op("act", lambda pt=pt, csl=csl: nc.scalar.activation(rtmp[:], denacc[:, pt, csl], AF.Ln),
                             reads=[("den", pt)], writes=["rtmp"])
                        p.op("act", lambda: nc.scalar.activation(rtmp[:], rtmp[:], AF.Exp, scale=-1.0),
                             reads=["rtmp"], writes=["rtmp"])
                        p.op("dve", lambda pt=pt, csl=csl: nc.vector.tensor_tensor(attn[:, pt, :], numacc[:, pt, csl], rtmp[:], ALU.mult),
                             reads=[("num", pt), "rtmp"], writes=[("attn", pt)])
                        p.op("act", lambda pt=pt: nc.scalar.activation(sqa[:], attn[:, pt, :], AF.Square),
                             reads=[("attn", pt)], writes=["sqa"])
                        p.op("pe", lambda pt=pt: nc.tensor.matmul(ps[6][:], ones_b[:], sqa[:], start=(pt == 0), stop=(pt == 2)),
                             reads=["ones_b", "sqa"], writes=[PS(6)] if pt == 0 else [])
                    p.lastw[PS(6)] = ("s_pe", p.cnt["pe"], "pe")
                    rstd_rep(rrep, ps[6][:], 384.0, PS(6), "rrep", rtmp, "rtmp")
                    for pt in range(3):
                        p.op("dve", lambda pt=pt: nc.vector.scalar_tensor_tensor(
                            atb[:, pt, :], attn[:, pt, :], gcols[:, G_ATT + pt:G_ATT + pt + 1], rrep[:], ALU.mult, ALU.mult),
                            reads=[("attn", pt), "gcols", "rrep"], writes=[("atb", pt)])
                        p.dma("sp", catT[pt, :, csl], atb[:, pt, :], reads=[("atb", pt)], writes=[("cat", pt, c4)])
                if "att" in debug:
                    p.barrier()
                p.barrier()
        if "att" in debug:
            with contextlib.ExitStack() as ph:
                tmpb = T(ph, "tmpb", [128, 2048], BF16)
                for pt in range(3):
                    p.dma("sp", tmpb[:], catT[pt, :, :], writes=["tmpb"])
                    p.dma("sp", dout("att%d" % pt, [128, 2048], BF16)[:, :], tmpb[:], reads=["tmpb"], writes=["o_att%d" % pt])
                p.barrier()
        ckpt("p2")
        if True:
            if True:
                pass

        with contextlib.ExitStack() as pb:
            LB = T(pb, "LB", [128, 7, 4, 128], BF16)
            LBs = T(pb, "LBs", [128, 7, 4, 128], BF16)
            L1 = T(pb, "L1", [128, 56, 64], BF16)
            L2 = T(pb, "L2", [128, 56, 64], BF16)
            lsc = T(pb, "lsc", [128, 12, 56], F32)
            state = T(pb, "state", [128, 56], F32)
            wglu = T(pb, "wglu", [128, 7, 7, 128], BF16)
            ABSA, FT, PH0 = 0, 1, 2
            for j in range(7):
                p.dma("pool", wglu[:, j, :, :], w_glu_in[j].rearrange("p (k m) -> p k m", k=7), writes=[("wglu", j)])
            with contextlib.ExitStack() as ph:
                al = T(ph, "al", [128, 3, 1792], F32)
                bx = T(ph, "bx", [128, 2, 1792], F32)
                ln_ = T(ph, "ln_", [128, 3, 56], F32)
                V = [T(ph, "vk%d" % i, [128, 56], F32) for i in range(6)]
                W = [T(ph, "wk%d" % i, [128, 1792], F32) for i in range(10)]
                p.dma("sp", al[:], alay_in.rearrange("p (a n) -> p a n", a=3), writes=["al"])
                p.dma("sp", bx[:], bexp_in.rearrange("p (a n) -> p a n", a=2), writes=["bx"])
                p.dma("sp", ln_[:], lanes_in.rearrange("p (a n) -> p a n", a=3), writes=["ln"])
                kk = [0]

                def ew(eng, fn, r, w_):
                    p.op(eng, fn, reads=r, writes=w_)

                def abar(are, aim, ldt, dt_, er, frac, co, si, tmp, tmp2, tag):
                    ew("act", lambda: nc.scalar.activation(dt_, ldt, AF.Exp), ["al", "ln"], [tag + "dt"])
                    ew("dve", lambda: nc.vector.tensor_tensor(tmp, are, dt_, ALU.mult), ["al", "ln", tag + "dt"], [tag + "tmp"])
                    ew("act", lambda: nc.scalar.activation(er, tmp, AF.Exp), [tag + "tmp"], [tag + "er"])
                    ew("dve", lambda: nc.vector.scalar_tensor_tensor(tmp, aim, 1.0 / (2 * PI), dt_, ALU.mult, ALU.mult), ["al", "ln", tag + "dt", tag + "er"], [tag + "tmp"])
                    ew("dve", lambda: nc.vector.tensor_scalar(tmp2, tmp, MAGIC, None, ALU.add), [tag + "tmp"], [tag + "tmp2"])
                    ew("dve", lambda: nc.vector.scalar_tensor_tensor(frac, tmp2, MAGIC, tmp, ALU.subtract, ALU.subtract), [tag + "tmp2", tag + "tmp"], [tag + "frac"])
                    ew("act", lambda: nc.scalar.activation(si, frac, AF.Sin, scale=-2.0 * PI), [tag + "frac"], [tag + "si"])
                    ew("act", lambda: nc.scalar.activation(tmp2, frac, AF.Abs), [tag + "frac", tag + "tmp2"], [tag + "tmp2"])
                    ew("act", lambda: nc.scalar.activation(co, tmp2, AF.Sin, scale=-2.0 * PI, bias=cc[:, 5:6]), [tag + "tmp2", "cc"], [tag + "co"])

                are, aim, ldt = al[:, 0, :], al[:, 1, :], al[:, 2, :]
                dt_, er, nfr, co, si, tmp, tmp2 = (W[i][:] for i in range(7))
                abar(are, aim, ldt, dt_, er, nfr, co, si, tmp, tmp2, "L")
                abr, abi, den = W[7][:], W[8][:], W[9][:]
                ew("dve", lambda: nc.vector.tensor_tensor(abr, er, co, ALU.mult), ["Ler", "Lco"], ["abr"])
                ew("dve", lambda: nc.vector.tensor_tensor(abi, er, si, ALU.mult), ["Ler", "Lsi"], ["abi"])
                ew("dve", lambda: nc.vector.tensor_scalar(abr, abr, -1.0, None, ALU.add), ["abr"], ["abr"])
                ew("dve", lambda: nc.vector.tensor_tensor(den, are, are, ALU.mult), ["al"], ["den"])
                ew("dve", lambda: nc.vector.tensor_tensor(tmp, aim, aim, ALU.mult), ["al", "Lco", "Ltmp"], ["Ltmp"])
                ew("dve", lambda: nc.vector.tensor_tensor(den, den, tmp, ALU.add), ["den", "Ltmp"], ["den"])
                ew("act", lambda: nc.scalar.activation(den, den, AF.Ln), ["den"], ["den"])
                ew("act", lambda: nc.scalar.activation(den, den, AF.Exp, scale=-1.0), ["den"], ["den"])
                nre, nim = co, si
                ew("dve", lambda: nc.vector.tensor_tensor(tmp, abr, are, ALU.mult), ["abr", "al", "Ltmp"], ["Ltmp"])
                ew("dve", lambda: nc.vector.tensor_tensor(tmp2, abi, aim, ALU.mult), ["abi", "al", "Ltmp2", "Lco"], ["Ltmp2"])
                ew("dve", lambda: nc.vector.tensor_tensor(nre, tmp, tmp2, ALU.add), ["Ltmp", "Ltmp2", "abr", "Lco"], ["nre"])
                ew("dve", lambda: nc.vector.tensor_tensor(tmp, abi, are, ALU.mult), ["abi", "al", "nre"], ["Ltmp"])
                ew("dve", lambda: nc.vector.tensor_tensor(tmp2, abr, aim, ALU.mult), ["abr", "al", "nre"], ["Ltmp2"])
                ew("dve", lambda: nc.vector.tensor_tensor(nim, tmp, tmp2, ALU.subtract), ["Ltmp", "Ltmp2", "abi", "Lsi"], ["nim"])
                ew("dve", lambda: nc.vector.tensor_tensor(nre, nre, den, ALU.mult), ["nre", "den"], ["nre"])
                ew("dve", lambda: nc.vector.tensor_tensor(nim, nim, den, ALU.mult), ["nim", "den"], ["nim"])
                bre, bim = bx[:, 0, :], bx[:, 1, :]
                bbr, bbi = W[7][:], W[8][:]
                ew("dve", lambda: nc.vector.tensor_tensor(tmp, nre, bre, ALU.mult), ["nre", "bx", "nim"], ["Ltmp"])
                ew("dve", lambda: nc.vector.tensor_tensor(tmp2, nim, bim, ALU.mult), ["nim", "bx", "nre"], ["Ltmp2"])
                ew("dve", lambda: nc.vector.tensor_tensor(bbr, tmp, tmp2, ALU.subtract), ["Ltmp", "Ltmp2", "abr", "nre", "nim"], ["bbr"])
                ew("dve", lambda: nc.vector.tensor_tensor(tmp, nre, bim, ALU.mult), ["nre", "bx", "bbr"], ["Ltmp"])
                ew("dve", lambda: nc.vector.tensor_tensor(tmp2, nim, bre, ALU.mult), ["nim", "bx", "bbr"], ["Ltmp2"])
                ew("dve", lambda: nc.vector.tensor_tensor(bbi, tmp, tmp2, ALU.add), ["Ltmp", "Ltmp2", "abi", "nim"], ["bbi"])
                v4 = lambda a: a.rearrange("p (o g n) -> p o g n", o=7, g=4)
                ew("dve", lambda: nc.vector.tensor_copy(LB[:, :, :, 0:64], v4(bbr)), ["bbr"], ["LB"])
                ew("dve", lambda: nc.vector.tensor_copy(LB[:, :, :, 64:128], v4(bbi)), ["bbi"], ["LB"])
                ew("dve", lambda: nc.vector.tensor_copy(LBs[:, :, :, 0:64], v4(bbi)), ["bbi"], ["LBs"])
                ew("dve", lambda: nc.vector.tensor_scalar(LBs[:, :, :, 64:128], v4(bbr), -1.0, None, ALU.mult), ["bbr"], ["LBs"])
                p.barrier()
            with contextlib.ExitStack() as ph:
                cx = T(ph, "cx", [128, 2, 3584], F32)
                ln_ = T(ph, "ln_b", [128, 3, 56], F32)
                V = [T(ph, "vkb%d" % i, [128, 56], F32) for i in range(6)]
                p.dma("sp", cx[:], cexp_in.rearrange("p (a n) -> p a n", a=2), writes=["cx"])
                p.dma("sp", ln_[:], lanes_in.rearrange("p (a n) -> p a n", a=3), writes=["ln"])
                c3 = lambda a: a.rearrange("p (g c) -> p g c", g=56)
                ew("dve", lambda: nc.vector.tensor_copy(L1[0:64, :, :], c3(cx[0:64, 0, :])), ["cx"], ["L1"])
                ew("dve", lambda: nc.vector.tensor_scalar(L1[64:128, :, :], c3(cx[64:128, 1, :]), -1.0, None, ALU.mult), ["cx"], ["L1"])
                ew("dve", lambda: nc.vector.tensor_scalar(L2[0:64, :, :], c3(cx[0:64, 1, :]), -1.0, None, ALU.mult), ["cx"], ["L2"])
                ew("dve", lambda: nc.vector.tensor_scalar(L2[64:128, :, :], c3(cx[64:128, 0, :]), -1.0, None, ALU.mult), ["cx"], ["L2"])
                dt2, er2, nfr2, co2, si2, tm2 = (V[i][:] for i in range(6))
                abar(ln_[:, 0, :], ln_[:, 1, :], ln_[:, 2, :], dt2, er2, nfr2, co2, si2, tm2, lsc[:, 11, :], "V")
                ew("dve", lambda: nc.vector.tensor_copy(lsc[:, ABSA, :], er2), ["Ver"], ["lsc"])
                ew("dve", lambda: nc.vector.tensor_scalar(lsc[:, FT, :], nfr2, -1.0, None, ALU.mult), ["Vfrac"], ["lsc"])
                ew("dve", lambda: nc.vector.tensor_scalar(tm2, lsc[:, FT, :], 512.0, MAGIC, ALU.mult, ALU.add), ["lsc", "Vtmp", "Vco"], ["Vtmp"])
                ew("dve", lambda: nc.vector.tensor_scalar(tm2, tm2, MAGIC, None, ALU.subtract), ["Vtmp"], ["Vtmp"])
                ew("dve", lambda: nc.vector.scalar_tensor_tensor(dt2, lsc[:, FT, :], 512.0, tm2, ALU.mult, ALU.subtract), ["lsc", "Vtmp", "Ver", "Vdt"], ["g512"])
                for c in range(8):
                    ew("dve", lambda c=c: nc.vector.tensor_scalar(tm2, dt2, float(c), MAGIC, ALU.mult, ALU.add), ["g512", "Vtmp", "lsc"], ["Vtmp"])
                    ew("dve", lambda: nc.vector.tensor_scalar(tm2, tm2, MAGIC, None, ALU.subtract), ["Vtmp"], ["Vtmp"])
                    ew("dve", lambda c=c: nc.vector.scalar_tensor_tensor(lsc[:, PH0 + c, :], dt2, float(c), tm2, ALU.mult, ALU.subtract), ["g512", "Vtmp"], ["lsc"])
                ew("dve", lambda: nc.vector.memset(state[:], 0.0), [], ["state"])
                if "ssmsetup" in debug:
                    for nm, t_, shp in (("LB", LB, [128, 7 * 4 * 128]), ("LBs", LBs, [128, 3584]), ("L1", L1, [128, 3584]), ("L2", L2, [128, 3584])):
                        p.dma("sp", dout(nm, shp, BF16)[:, :], t_[:].rearrange("p a b c -> p (a b c)") if nm in ("LB", "LBs") else t_[:].rearrange("p a b -> p (a b)"), reads=[nm], writes=["o_" + nm])
                    p.dma("sp", dout("lsc", [128, 12 * 56])[:, :], lsc[:].rearrange("p a b -> p (a b)"), reads=["lsc"], writes=["o_lsc"])
                p.barrier()
                ckpt("p1b0")

            with contextlib.ExitStack() as ph:
                uT = T(ph, "uT", [128, 7, NT], BF16)
                NB2 = 2
                tu = [T(ph, "tu%d" % i, [128, NT], F32) for i in range(3)]
                tv = [T(ph, "tv%d" % i, [128, NT], F32) for i in range(3)]
                tS = [T(ph, "tS%d" % i, [128, NT], F32) for i in range(3)]
                tC = [T(ph, "tC%d" % i, [128, NT], F32) for i in range(3)]
                ta = [T(ph, "ta%d" % i, [128, NT], F32) for i in range(NB2)]
                tb_ = [T(ph, "tb%d" % i, [128, NT], F32) for i in range(NB2)]
                tw = [T(ph, "tw%d" % i, [128, NT], F32) for i in range(NB2)]
                M1 = [T(ph, "M1_%d" % i, [128, NT], BF16) for i in range(NB2)]
                M2 = [T(ph, "M2_%d" % i, [128, NT], BF16) for i in range(NB2)]
                ygb = T(ph, "ygb", [128, 7, NT], BF16)
                ssm = T(ph, "ssm", [128, 7, NT], F32)
                yv = [T(ph, "yv%d" % i, [128, NT], F32) for i in range(2)]
                sg = [T(ph, "sg%d" % i, [128, NT], F32) for i in range(2)]
                sqs = [T(ph, "sqs%d" % i, [128, NT], BF16) for i in range(2)]
                rtmp = T(ph, "rtmp2", [128, NT], F32)
                rrep = T(ph, "rrep2", [128, NT], F32)
                uTs = [uT, T(ph, "uT1", [128, 7, NT], BF16)]

                def load_u(ci_):
                    p.dma("sp", uTs[ci_ % 2][:], uT_scr[:, :, ci_ * NT:(ci_ + 1) * NT].rearrange("o p n -> p o n"),
                          writes=[("uT", ci_ % 2, o) for o in range(7)])

                load_u(0)
                for ci in range(NCH):
                    own = ci >= 4
                    par = ci % 2
                    uT = uTs[par]
                    nxt = []
                    if ci + 1 < NCH:
                        load_u(ci + 1)
                    items = [(o, hb, gq) for o in range(7) for hb in range(2) for gq in range(4)]

                    def stageAmm(idx):
                        o, hb, gq = items[idx]
                        rows = slice(64 * hb, 64 * hb + 64)
                        s2 = idx % 2
                        bA, bB = 3 + 2 * s2, 4 + 2 * s2
                        p.op("pe", lambda: nc.tensor.matmul(ps[bA][:], LB[rows, o, gq, :], uT[rows, o, :], start=True, stop=True),
                             reads=["LB", ("uT", par, o)], writes=[PS(bA)])
                        p.op("pe", lambda: nc.tensor.matmul(ps[bB][:], LBs[rows, o, gq, :], uT[rows, o, :], start=True, stop=True),
                             reads=["LBs", ("uT", par, o)], writes=[PS(bB)])

                    def stageAu(idx):
                        o, hb, gq = items[idx]
                        g = 8 * o + 4 * hb + gq
                        s3 = idx % 3
                        p.op("pool", lambda: nc.gpsimd.tensor_scalar(
                            tu[s3][:], jidx[:], lsc[:, FT, g:g + 1], lsc[:, PH0 + ci, g:g + 1], ALU.mult, ALU.add),
                            reads=["jidx", "lsc"], writes=[("tu", s3)])

                    def stageA(idx):
                        s3 = idx % 3
                        p.op("dve", lambda: nc.vector.tensor_scalar(tv[s3][:], tu[s3][:], MAGIC, None, ALU.add),
                             reads=[("tu", s3)], writes=[("tv", s3)])
                        p.op("dve", lambda: nc.vector.scalar_tensor_tensor(tv[s3][:], tv[s3][:], MAGIC, tu[s3][:], ALU.subtract, ALU.subtract),
                             reads=[("tv", s3), ("tu", s3)], writes=[("tv", s3)])
                        p.op("act", lambda: nc.scalar.activation(tS[s3][:], tv[s3][:], AF.Sin, scale=-2.0 * PI),
                             reads=[("tv", s3)], writes=[("tS", s3)])
                        p.op("act", lambda: nc.scalar.activation(tu[s3][:], tv[s3][:], AF.Abs),
                             reads=[("tv", s3)], writes=[("tu", s3)])
                        p.op("act", lambda: nc.scalar.activation(tC[s3][:], tu[s3][:], AF.Sin, scale=-2.0 * PI, bias=cc[:, 5:6]),
                             reads=[("tu", s3), "cc"], writes=[("tC", s3)])

                    def stageB(idx):
                        o, hb, gq = items[idx]
                        g = 8 * o + 4 * hb + gq
                        rows = slice(64 * hb, 64 * hb + 64)
                        s3 = idx % 3
                        s_ = idx % 2
                        bA, bB = 3 + 2 * s_, 4 + 2 * s_
                        p.op("dve", lambda: nc.vector.tensor_tensor(ta[s_][:], ps[bA][:], tC[s3][:], ALU.mult),
                             reads=[PS(bA), ("tC", s3)], writes=[("ta", s_)])
                        p.op("dve", lambda: nc.vector.tensor_tensor(tb_[s_][:], ps[bB][:], tS[s3][:], ALU.mult),
                             reads=[PS(bB), ("tS", s3)], writes=[("tb", s_)])
                        p.op("dve", lambda: nc.vector.tensor_tensor(ta[s_][:], ta[s_][:], tb_[s_][:], ALU.add),
                             reads=[("ta", s_), ("tb", s_)], writes=[("ta", s_)])
                        p.op("dve", lambda: nc.vector.tensor_tensor_scan(
                            tw[s_][:], lsc[:, ABSA, g:g + 1].to_broadcast([128, NT]), ta[s_][:], state[:, g:g + 1], ALU.mult, ALU.add),
                            reads=[("ta", s_), "lsc", ("state", g)], writes=[("tw", s_)])
                        p.op("pool", lambda: nc.gpsimd.tensor_copy(state[:, g:g + 1], tw[s_][:, NT - 1:NT]),
                             reads=[("tw", s_)], writes=[("state", g)])
                        if own:
                            p.op("dve", lambda: nc.vector.tensor_tensor(M1[s_][:], tC[s3][:], tw[s_][:], ALU.mult),
                                 reads=[("tC", s3), ("tw", s_)], writes=[("M1", s_)])
                            p.op("dve", lambda: nc.vector.tensor_tensor(M2[s_][:], tS[s3][:], tw[s_][:], ALU.mult),
                                 reads=[("tS", s3), ("tw", s_)], writes=[("M2", s_)])
                            yk = ("ps", 7, hb)
                            p.op("pe", lambda: nc.tensor.matmul(ps[7][rows, :], L1[:, g, :], M1[s_][:], start=(gq == 0), stop=False),
                                 reads=["L1", ("M1", s_)], writes=[yk] if gq == 0 else [], sig=False)
                            p.op("pe", lambda: nc.tensor.matmul(ps[7][rows, :], L2[:, g, :], M2[s_][:], start=False, stop=(gq == 3)),
                                 reads=["L2", ("M2", s_)], writes=[], sig=True)
                            if gq == 3:
                                p.lastw[yk] = ("s_pe", p.cnt["pe"], "pe")
                            if hb == 1 and gq == 3:
                                y_ = yv[o % 2]
                                p.op("dve", lambda: nc.vector.scalar_tensor_tensor(
                                    y_[:], uT[:, o, :], gcols[:, DSK + o:DSK + o + 1], ps[7][:], ALU.mult, ALU.add),
                                    reads=[("uT", par, o), "gcols", ("ps", 7, 0), ("ps", 7, 1)], writes=[("yv", o % 2)])
                                p.op("act", lambda: nc.scalar.activation(ygb[:, o, :], y_[:], AF.Gelu_apprx_tanh),
                                     reads=[("yv", o % 2)], writes=[("ygb", o)])

                    stageAu(0)
                    stageAu(1)
                    stageAu(2)
                    stageA(0)
                    stageA(1)
                    stageAmm(0)
                    for idx in range(len(items)):
                        if nxt and (idx % 3 != 2):
                            nxt.pop(0)()
                        if idx + 3 < len(items):
                            stageAu(idx + 3)
                        if idx + 2 < len(items):
                            stageA(idx + 2)
                        if idx + 1 < len(items):
                            stageAmm(idx + 1)
                        stageB(idx)
                        if idx % 4 == 3:
                            conv_step()
                    while nxt:
                        nxt.pop(0)()
                    if own:
                        c4 = ci - 4
                        csl = slice(c4 * NT, (c4 + 1) * NT)
                        for mt in range(7):
                            for kt in range(7):
                                p.op("pe", lambda mt=mt, kt=kt: nc.tensor.matmul(ps[2][:], wglu[:, mt, kt, :], ygb[:, kt, :], start=(kt == 0), stop=(kt == 6)),
                                     reads=[("wglu", mt), ("ygb", kt)], writes=[PS(2)] if kt == 0 else [], sig=(kt == 6))
                            p.lastw[PS(2)] = ("s_pe", p.cnt["pe"], "pe")
                            s2 = sg[mt % 2]
                            q2 = sqs[mt % 2]
                            p.op("act", lambda mt=mt, s2=s2: nc.scalar.activation(s2[:], ps[2][:], AF.Sigmoid, bias=gcols[:, B_GLU + mt:B_GLU + mt + 1]),
                                 reads=[PS(2), "gcols"], writes=[("sg", mt % 2)])
                            p.op("dve", lambda mt=mt, s2=s2: nc.vector.tensor_tensor(ssm[:, mt, :], ygb[:, mt, :], s2[:], ALU.mult),
                                 reads=[("ygb", mt), ("sg", mt % 2)], writes=[("ssm", mt)])
                            p.op("act", lambda mt=mt, q2=q2: nc.scalar.activation(q2[:], ssm[:, mt, :], AF.Square),
                                 reads=[("ssm", mt)], writes=[("sqs", mt % 2)])
                            p.op("pe", lambda mt=mt, q2=q2: nc.tensor.matmul(ps[0][:], ones_b[:], q2[:], start=(mt == 0), stop=(mt == 6)),
                                 reads=["ones_b", ("sqs", mt % 2)], writes=[PS(0)] if mt == 0 else [])
                        p.lastw[PS(0)] = ("s_pe", p.cnt["pe"], "pe")
                        rstd_rep(rrep, ps[0][:], 896.0, PS(0), "rrep2", rtmp, "rtmp2")
                        for mt in range(7):
                            p.op("dve", lambda mt=mt: nc.vector.scalar_tensor_tensor(
                                ygb[:, mt, :], ssm[:, mt, :], gcols[:, G_SSM + mt:G_SSM + mt + 1], rrep[:], ALU.mult, ALU.mult),
                                reads=[("ssm", mt), "gcols", "rrep2"], writes=[("ygb", mt)])
                            p.dma("sp", catT[3 + mt, :, csl], ygb[:, mt, :], reads=[("ygb", mt)], writes=[("cat", 3 + mt, c4)])
                p.barrier()
        if "ssm" in debug:
            with contextlib.ExitStack() as ph:
                tmpb = T(ph, "tmpb2", [128, 2048], BF16)
                for mt in range(7):
                    p.dma("sp", tmpb[:], catT[3 + mt, :, :], writes=["tmpb"])
                    p.dma("sp", dout("ssm%d" % mt, [128, 2048], BF16)[:, :], tmpb[:], reads=["tmpb"], writes=["o_ssm%d" % mt])
                p.barrier()
        ckpt("p1b")

        with contextlib.ExitStack() as ph:
            xT = T(ph, "xT", [128, 16, NT], F32)
            xt = [T(ph, "xt3_%d" % i, [128, D], F32) for i in range(2)]
            yT = T(ph, "yT", [128, 16, NT], F32)
            yTb = yT[:].rearrange("p a b -> p (a b)").bitcast(BF16)
            h2T = yTb[:, 0:8192].rearrange("p (a b) -> p a b", a=16)
            catc = yTb[:, 8192:8192 + 5120].rearrange("p (a b) -> p a b", a=10)
            aT = T(ph, "aT", [128, 64, NT], BF16)
            mixT = aT[:].rearrange("p a b -> p (a b)").bitcast(F32)[:, 0:8192].rearrange("p (a b) -> p a b", a=16)
            wb = [T(ph, "wb%d" % i, [128, 8192], BF16) for i in range(2)]
            rtmp = T(ph, "rtmp3", [128, NT], F32)
            rrep = T(ph, "rrep3", [128, NT], F32)
            sq3 = [T(ph, "sq3_%d" % i, [128, NT], BF16) for i in range(2)]
            tm3 = [T(ph, "tm3_%d" % i, [128, NT], F32) for i in range(2)]
            rl3 = [T(ph, "rl3_%d" % i, [128, NT], BF16) for i in range(2)]
            ALL_H2 = [("h2T", f) for f in range(16)] + [("catc", k) for k in range(10)]
            ALL_A = [("aT", j) for j in range(64)]
            wl = 0
            for c4 in range(4):
                csl = slice(c4 * NT, (c4 + 1) * NT)
                for tt in range(4):
                    row0 = c4 * NT + tt * 128
                    xb_ = xt[tt % 2]
                    p.dma("sp", xb_[:], x_own[row0:row0 + 128, :], writes=[("xt", tt % 2)])
                    for f4 in range(4):
                        bank = f4 % 2
                        for i in range(4):
                            f = f4 * 4 + i
                            p.op("pe", lambda xb_=xb_, f=f, i=i, bank=bank: nc.tensor.transpose(
                                ps[bank][:, i * 128:(i + 1) * 128], xb_[:, f * 128:(f + 1) * 128], ident_f[:]),
                                reads=[("xt", tt % 2), "ident_f"], writes=[PS(bank)] if i == 0 else [], sig=(i == 3))
                        p.lastw[PS(bank)] = ("s_pe", p.cnt["pe"], "pe")
                        p.op("dve", lambda f4=f4, tt=tt, bank=bank: nc.vector.tensor_copy(
                            xT[:, f4 * 4:f4 * 4 + 4, tt * 128:(tt + 1) * 128], ps[bank][:].rearrange("p (a b) -> p a b", a=4)),
                            reads=[PS(bank)], writes=[("xT", f4 * 4 + i) for i in range(4)])
                for k in range(10):
                    p.dma("sp", catc[:, k, :], catT[k, :, csl], reads=[("cat", k, c4)], writes=[("catc", k)] + [("yT", f) for f in range(16)])
                for mg in range(4):
                    w = wb[wl % 2]
                    wkey = ("wb", wl % 2)
                    wl += 1
                    wv = w[:].rearrange("p (j x) -> p j x", j=4)[:, :, 0:1280].rearrange("p j (k m) -> p j k m", k=10)
                    for j in range(4):
                        p.dma("sp", wv[:, j, :, :], w_out_b[mg * 4 + j].rearrange("p (k m) -> p k m", k=10), writes=[(wkey, j)])
                    for j in range(4):
                        mt = mg * 4 + j
                        bank = 2 + (mt % 2)
                        for kt in range(10):
                            p.op("pe", lambda wv=wv, j=j, kt=kt, bank=bank: nc.tensor.matmul(ps[bank][:], wv[:, j, kt, :], catc[:, kt, :], start=(kt == 0), stop=(kt == 9)),
                                 reads=[(wkey, j), ("catc", kt)], writes=[PS(bank)] if kt == 0 else [], sig=(kt == 9))
                        p.lastw[PS(bank)] = ("s_pe", p.cnt["pe"], "pe")
                        p.op("dve", lambda mt=mt, bank=bank: nc.vector.tensor_copy(mixT[:, mt, :], ps[bank][:]),
                             reads=[PS(bank)], writes=[("mixT", mt)] + ALL_A)
                        q2 = sq3[mt % 2]
                        p.op("act", lambda mt=mt, bank=bank, q2=q2: nc.scalar.activation(q2[:], ps[bank][:], AF.Square),
                             reads=[PS(bank)], writes=[("sq3", mt % 2)])
                        p.op("pe", lambda mt=mt, q2=q2: nc.tensor.matmul(ps[4][:], ones_b[:], q2[:], start=(mt == 0), stop=(mt == 15)),
                             reads=["ones_b", ("sq3", mt % 2)], writes=[PS(4)] if mt == 0 else [])
                p.lastw[PS(4)] = ("s_pe", p.cnt["pe"], "pe")
                rstd_rep(rrep, ps[4][:], float(D), PS(4), "rrep3", rtmp, "rtmp3")
                for f in range(16):
                    t_ = tm3[f % 2]
                    q2 = sq3[f % 2]
                    p.op("dve", lambda f=f, t_=t_: nc.vector.tensor_tensor(t_[:], mixT[:, f, :], rrep[:], ALU.mult),
                         reads=[("mixT", f), "rrep3"], writes=[("tm3", f % 2)])
                    p.op("dve", lambda f=f, t_=t_: nc.vector.scalar_tensor_tensor(xT[:, f, :], t_[:], modc[:, G1 + f:G1 + f + 1], xT[:, f, :], ALU.mult, ALU.add),
                         reads=[("tm3", f % 2), "modc", ("xT", f)], writes=[("xT", f)])
                    p.op("act", lambda f=f, q2=q2: nc.scalar.activation(q2[:], xT[:, f, :], AF.Square),
                         reads=[("xT", f)], writes=[("sq3", f % 2)])
                    p.op("pe", lambda f=f, q2=q2: nc.tensor.matmul(ps[5][:], ones_b[:], q2[:], start=(f == 0), stop=(f == 15)),
                         reads=["ones_b", ("sq3", f % 2)], writes=[PS(5)] if f == 0 else [])
                p.lastw[PS(5)] = ("s_pe", p.cnt["pe"], "pe")
                rstd_rep(rrep, ps[5][:], float(D), PS(5), "rrep3", rtmp, "rtmp3")
                for f in range(16):
                    t_ = tm3[f % 2]
                    p.op("dve", lambda f=f, t_=t_: nc.vector.scalar_tensor_tensor(t_[:], xT[:, f, :], modc[:, A2 + f:A2 + f + 1], rrep[:], ALU.mult, ALU.mult),
                         reads=[("xT", f), "modc", "rrep3"], writes=[("tm3", f % 2)])
                    p.op("act", lambda f=f, t_=t_: nc.scalar.activation(h2T[:, f, :], t_[:], AF.Identity, bias=modc[:, B2 + f:B2 + f + 1]),
                         reads=[("tm3", f % 2), "modc"], writes=[("h2T", f)] + [("yT", i) for i in range(16)])
                for jg in range(16):
                    w = wb[wl % 2]
                    wkey = ("wb", wl % 2)
                    wl += 1
                    wv = w[:].rearrange("p (j k m) -> p j k m", j=4, k=16)
                    for j in range(4):
                        p.dma("pool" if j % 2 else "sp", wv[:, j, :, :], w_m1_b[jg * 4 + j].rearrange("p (k m) -> p k m", k=16), writes=[(wkey, j)])
                    for j in range(4):
                        jt = jg * 4 + j
                        bank = 2 + (jt % 2)
                        for kt in range(16):
                            p.op("pe", lambda wv=wv, j=j, kt=kt, bank=bank: nc.tensor.matmul(ps[bank][:], wv[:, j, kt, :], h2T[:, kt, :], start=(kt == 0), stop=(kt == 15)),
                                 reads=[(wkey, j), ("h2T", kt)], writes=[PS(bank)] if kt == 0 else [], sig=(kt == 15))
                        p.lastw[PS(bank)] = ("s_pe", p.cnt["pe"], "pe")
                        r_ = rl3[jt % 2]
                        p.op("act", lambda bank=bank, r_=r_: nc.scalar.activation(r_[:], ps[bank][:], AF.Relu),
                             reads=[PS(bank)], writes=[("rl3", jt % 2)])
                        p.op("dve", lambda jt=jt, r_=r_: nc.vector.tensor_tensor(aT[:, jt, :], r_[:], r_[:], ALU.mult),
                             reads=[("rl3", jt % 2)], writes=[("aT", jt)] + ([("mixT", f) for f in range(16)] if jt < 32 else []))
                for mt in range(16):
                    w = wb[wl % 2]
                    wkey = ("wb", wl % 2)
                    wl += 1
                    wv = w[:].rearrange("p (j m) -> p j m", j=64)
                    p.dma("pool" if mt % 2 else "sp", wv, w_m2_b[mt].rearrange("p (j m) -> p j m", j=64), writes=[(wkey, j_) for j_ in range(4)])
                    bank = 2 + (mt % 2)
                    for jt in range(64):
                        p.op("pe", lambda wv=wv, jt=jt, bank=bank: nc.tensor.matmul(ps[bank][:], wv[:, jt, :], aT[:, jt, :], start=(jt == 0), stop=(jt == 63)),
                             reads=[(wkey, 0), (wkey, 1), (wkey, 2), (wkey, 3), ("aT", jt)], writes=[PS(bank)] if jt == 0 else [], sig=(jt == 63))
                    p.lastw[PS(bank)] = ("s_pe", p.cnt["pe"], "pe")
                    p.op("dve", lambda mt=mt, bank=bank: nc.vector.tensor_copy(yT[:, mt, :], ps[bank][:]),
                         reads=[PS(bank)], writes=[("yT", mt)] + ALL_H2)
                    q2 = sq3[mt % 2]
                    p.op("act", lambda bank=bank, q2=q2: nc.scalar.activation(q2[:], ps[bank][:], AF.Square),
                         reads=[PS(bank)], writes=[("sq3", mt % 2)])
                    p.op("pe", lambda mt=mt, q2=q2: nc.tensor.matmul(ps[4][:], ones_b[:], q2[:], start=(mt == 0), stop=(mt == 15)),
                         reads=["ones_b", ("sq3", mt % 2)], writes=[PS(4)] if mt == 0 else [])
                p.lastw[PS(4)] = ("s_pe", p.cnt["pe"], "pe")
                rstd_rep(rrep, ps[4][:], float(D), PS(4), "rrep3", rtmp, "rtmp3")
                for f in range(16):
                    t_ = tm3[f % 2]
                    p.op("dve", lambda f=f, t_=t_: nc.vector.tensor_tensor(t_[:], yT[:, f, :], rrep[:], ALU.mult),
                         reads=[("yT", f), "rrep3"], writes=[("tm3", f % 2)])
                    p.op("dve", lambda f=f, t_=t_: nc.vector.scalar_tensor_tensor(xT[:, f, :], t_[:], modc[:, G2 + f:G2 + f + 1], xT[:, f, :], ALU.mult, ALU.add),
                         reads=[("tm3", f % 2), "modc", ("xT", f)], writes=[("xT", f)])
                for tt in range(4):
                    row0 = c4 * NT + tt * 128
                    ob = xt[tt % 2]
                    for f4 in range(4):
                        bank = f4 % 2
                        for i in range(4):
                            f = f4 * 4 + i
                            p.op("pe", lambda f=f, i=i, bank=bank, tt=tt: nc.tensor.transpose(
                                ps[bank][:, i * 128:(i + 1) * 128], xT[:, f, tt * 128:(tt + 1) * 128], ident_f[:]),
                                reads=[("xT", f), "ident_f"], writes=[PS(bank)] if i == 0 else [], sig=(i == 3))
                        p.lastw[PS(bank)] = ("s_pe", p.cnt["pe"], "pe")
                        p.op("act", lambda f4=f4, ob=ob, bank=bank: nc.scalar.copy(ob[:, f4 * 512:(f4 + 1) * 512], ps[bank][:]),
                             reads=[PS(bank)], writes=[("xt", tt % 2)])
                    p.dma("sp", out[row0:row0 + 128, :], ob[:], reads=[("xt", tt % 2)], writes=[("out", c4, tt)])


def _host_layouts(inp):
    f32 = np.float32
    L = {}
    w_ada = inp["w_ada"][0]
    L["w_ada_t"] = np.ascontiguousarray(w_ada.reshape(16, 128, 24, 512).transpose(2, 1, 0, 3)).reshape(24, 128, 16 * 512)
    L["b_ada_col"] = np.ascontiguousarray(inp["b_ada"][0].reshape(96, 128).T)
    col16 = lambda v: v.reshape(-1, 128).T
    L["gcols"] = np.ascontiguousarray(np.concatenate([
        col16(inp["g_pre_mix"][0]), col16(inp["g_post_mix"][0]), col16(inp["g_pre_mlp"][0]), col16(inp["g_post_mlp"][0]),
        col16(inp["g_attn_out"][0]), col16(inp["g_ssm_out"][0]), col16(inp["b_glu"][0]), col16(inp["ssm_d"][0].reshape(-1))], axis=1).astype(f32))
    L["w_in_t"] = np.ascontiguousarray(inp["w_in"][0].reshape(16, 128, 22, 128).transpose(2, 1, 0, 3)).reshape(22, 128, 2048)
    L["w_out_t"] = np.ascontiguousarray(inp["w_out"][0].reshape(10, 128, 16, 128).transpose(2, 1, 0, 3)).reshape(16, 128, 1280)
    L["w_m1_t"] = np.ascontiguousarray(inp["w_mlp_in"][0].reshape(16, 128, 64, 128).transpose(2, 1, 0, 3)).reshape(64, 128, 2048)
    L["w_m2_t"] = np.ascontiguousarray(inp["w_mlp_out"][0].reshape(64, 128, 16, 128).transpose(2, 1, 0, 3)).reshape(16, 128, 8192)
    L["w_glu_t"] = np.ascontiguousarray(inp["w_glu"][0].reshape(7, 128, 7, 128).transpose(2, 1, 0, 3)).reshape(7, 128, 896)
    a_re, a_im, ldt = inp["ssm_a_re"][0], inp["ssm_a_im"][0], inp["ssm_log_dt"][0]
    b_re, b_im = inp["ssm_b_re"][0], inp["ssm_b_im"][0]
    c_re, c_im = inp["ssm_c_re"][0], inp["ssm_c_im"][0]
    alay = np.zeros((128, 3, 7, 4, 64), f32)
    bexp = np.zeros((128, 2, 7, 4, 64), f32)
    cexp = np.zeros((128, 2, 56, 64), f32)
    for o in range(7):
        for hb in range(2):
            for gq in range(4):
                g = 8 * o + 4 * hb + gq
                rows = slice(64 * hb, 64 * hb + 64)
                alay[rows, 0, o, gq, :] = a_re[g][None, :]
                alay[rows, 1, o, gq, :] = a_im[g][None, :]
                alay[rows, 2, o, gq, :] = ldt[g]
                r0 = 64 * hb + 16 * gq
                bexp[r0:r0 + 16, 0, o, gq, :] = b_re[g].T
                bexp[r0:r0 + 16, 1, o, gq, :] = b_im[g].T
                for half in range(2):
                    lr = slice(64 * half, 64 * half + 64)
                    cexp[lr, 0, g, 16 * gq:16 * gq + 16] = c_re[g].T
                    cexp[lr, 1, g, 16 * gq:16 * gq + 16] = c_im[g].T
    L["alay"] = alay.reshape(128, -1)
    L["bexp"] = bexp.reshape(128, -1)
    L["cexp"] = cexp.reshape(128, -1)
    lanes = np.zeros((128, 3, 56), f32)
    lanes[:, 0, :] = np.concatenate([a_re.T, a_re.T], 0)
    lanes[:, 1, :] = np.concatenate([a_im.T, a_im.T], 0)
    lanes[:, 2, :] = ldt[None, :]
    L["lanes"] = lanes.reshape(128, -1)
    L["ident"] = np.eye(128, dtype=f32)
    e = np.arange(128) % 64
    rm = np.zeros((128, 128), f32)
    for m in range(128):
        if e[m] < 8:
            rm[m + 8, m] = 1.0
        elif e[m] < 16:
            rm[m - 8, m] = 1.0
    L["rmat"] = rm
    kk = np.arange(128)[:, None]
    qq = np.arange(128)[None, :]
    L["tri_cur"] = np.where(kk <= qq, 1.0, 0.0).astype(f32)
    L["tri_prev"] = np.where(kk >= qq, 1.0, 0.0).astype(f32)
    L["jidx"] = np.broadcast_to(np.arange(NT, dtype=f32)[None, :], (128, NT)).copy()
    freq = (500000.0 ** (-(2.0 * (e % 8)) / 16.0)).astype(np.float64)
    cc = np.zeros((128, 8), f32)
    cc[:, 0] = (freq / (2 * np.pi)).astype(f32)
    cc[:, 1] = (e < 16)
    cc[:, 2] = 1.0 - cc[:, 1]
    cc[:, 3] = np.where(e < 8, -1.0, np.where(e < 16, 1.0, 0.0))
    cc[:, 5] = PI / 2
    L["cc"] = cc
    return L


_CACHE = {}


def kernel(**inputs):
    inp = {k: np.asarray(v) for k, v in inputs.items()}
    if "nc" not in _CACHE:
        _CACHE["nc"] = build_program()[0]
    nc = _CACHE["nc"]
    L = _host_layouts(inp)
    x = inp["x"]
    pos = inp["positions"].astype(np.int32)
    in_maps = []
    shared = {k: L[k] for k in ("w_ada_t", "b_ada_col", "gcols", "w_in_t", "w_out_t", "w_m1_t", "w_m2_t", "w_glu_t",
                                "alay", "bexp", "cexp", "lanes", "ident", "rmat", "jidx")}
    for core in range(8):
        b, h = core // 2, core % 2
        m = dict(shared)
        m["x_own"] = np.ascontiguousarray(x[b, h * 2048:(h + 1) * 2048])
        m["x_prev"] = np.ascontiguousarray(x[b, 0:2048]) if h == 1 else np.zeros((2048, D), np.float32)
        pw = np.zeros((1, 4096), np.int32)
        pw[0, 2048:] = pos[b, h * 2048:(h + 1) * 2048]
        if h == 1:
            pw[0, :2048] = pos[b, 0:2048]
        m["pos"] = pw
        m["c_col"] = np.ascontiguousarray(inp["c"][b].reshape(16, 128).T)
        cc = L["cc"].copy()
        cc[:, 4] = float(h)
        m["cc"] = cc
        prev0 = L["tri_prev"] if h == 1 else np.zeros((128, 128), np.float32)
        m["masks"] = np.ascontiguousarray(np.concatenate([L["tri_prev"], L["tri_cur"], prev0, L["tri_cur"]], axis=1))
        in_maps.append(m)
    res = run_bass_kernel_spmd(nc, in_maps, core_ids=list(range(8)))
    outp = np.zeros((4, 4096, D), np.float32)
    for core in range(8):
        b, h = core // 2, core % 2
        outp[b, h * 2048:(h + 1) * 2048] = res.results[core]["out"]
    return outp
```
